# Optimizing a Trainium2 kernel written in Bass

```python
import math
import jax, jax.numpy as jnp
from jax import lax
import numpy as np

D_MODEL = 2048
BATCH = 2
SEQ = 16384
DEPTH = 1

N_Q_HEADS = 16
N_KV_HEADS = 4
HEAD_DIM = 64
GROUP = N_Q_HEADS // N_KV_HEADS
Q_DIM = N_Q_HEADS * HEAD_DIM
KV_DIM = N_KV_HEADS * HEAD_DIM
WINDOW = 128
BLOCK = 128
CONV_CH = D_MODEL - Q_DIM
CONV_K = 31
MIX_DIM = Q_DIM + CONV_CH
IN_COLS = Q_DIM + 2 * KV_DIM + 2 * CONV_CH
D_FF = 5632
FFN_CONV_K = 3
EPS = 1e-6

kernel_name = "hymba_swa_sink_conformer_convffn"


def rms_norm(x, w):
    xf = x.astype(jnp.float32)
    y = xf * lax.rsqrt(jnp.mean(xf * xf, axis=-1, keepdims=True) + EPS)
    return (y * w.astype(jnp.float32)).astype(x.dtype)


def layer_norm(x, w, b):
    xf = x.astype(jnp.float32)
    mu = jnp.mean(xf, axis=-1, keepdims=True)
    var = jnp.mean(jnp.square(xf - mu), axis=-1, keepdims=True)
    y = (xf - mu) * lax.rsqrt(var + EPS)
    return (y * w.astype(jnp.float32) + b.astype(jnp.float32)).astype(x.dtype)


def causal_depthwise_conv(x, w, b):
    k, c = w.shape
    y = lax.conv_general_dilated(
        x, w[:, None, :].astype(x.dtype), window_strides=(1,), padding=[(k - 1, 0)],
        dimension_numbers=("NWC", "WIO", "NWC"), feature_group_count=c)
    return y + b.astype(x.dtype)


def sliding_window_sink_attention(q, k, v, sinks):
    b, s = q.shape[0], q.shape[1]
    nb = s // BLOCK
    qb = q.reshape(b, nb, BLOCK, N_KV_HEADS, GROUP, HEAD_DIM)

    def band(t):
        tp = jnp.pad(t, ((0, 0), (BLOCK, 0), (0, 0), (0, 0)))[:, :s]
        prev = tp.reshape(b, nb, BLOCK, N_KV_HEADS, HEAD_DIM)
        cur = t.reshape(b, nb, BLOCK, N_KV_HEADS, HEAD_DIM)
        return jnp.concatenate([prev, cur], axis=2)

    kb, vb = band(k), band(v)
    scale = 1.0 / math.sqrt(HEAD_DIM)
    scores = jnp.einsum("bnqhgd,bnkhd->bnhgqk", qb, kb,
                        preferred_element_type=jnp.float32) * scale
    qi = jnp.arange(BLOCK)[:, None] + BLOCK
    kj = jnp.arange(2 * BLOCK)[None, :]
    diff = qi - kj
    in_window = (diff >= 0) & (diff < WINDOW)
    key_pos = jnp.arange(nb)[:, None] * BLOCK - BLOCK + jnp.arange(2 * BLOCK)[None, :]
    valid = in_window[None] & (key_pos >= 0)[:, None, :]
    scores = jnp.where(valid[None, :, None, None], scores, -jnp.inf)
    sink = jnp.broadcast_to(sinks.astype(jnp.float32).reshape(1, 1, N_KV_HEADS, GROUP, 1, 1),
                            scores.shape[:-1] + (1,))
    probs = jax.nn.softmax(jnp.concatenate([scores, sink], axis=-1), axis=-1)[..., :-1]
    out = jnp.einsum("bnhgqk,bnkhd->bnqhgd", probs.astype(v.dtype), vb)
    return out.reshape(b, s, Q_DIM)


def hybrid_layer(x, attn_norm_w, w_in, q_norm_w, k_norm_w, sinks, conv_dw_w, conv_dw_b,
                 conv_ln_w, conv_ln_b, w_out, ffn_norm_w, w_up, ffn_dw_w, ffn_dw_b, w_down):
    b, s, _ = x.shape
    h = rms_norm(x, attn_norm_w)
    p = h @ w_in
    q, k, v, conv_in = jnp.split(p, [Q_DIM, Q_DIM + KV_DIM, Q_DIM + 2 * KV_DIM], axis=-1)
    q = rms_norm(q.reshape(b, s, N_KV_HEADS, GROUP, HEAD_DIM), q_norm_w)
    k = rms_norm(k.reshape(b, s, N_KV_HEADS, HEAD_DIM), k_norm_w)
    v = v.reshape(b, s, N_KV_HEADS, HEAD_DIM)
    attn_out = sliding_window_sink_attention(q, k, v, sinks)
    a, g = jnp.split(conv_in, 2, axis=-1)
    c = a * jax.nn.sigmoid(g)
    c = causal_depthwise_conv(c, conv_dw_w, conv_dw_b)
    c = jax.nn.silu(layer_norm(c, conv_ln_w, conv_ln_b))
    x = x + jnp.concatenate([attn_out, c], axis=-1) @ w_out
    h = rms_norm(x, ffn_norm_w)
    u = causal_depthwise_conv(h @ w_up, ffn_dw_w, ffn_dw_b)
    gate, val = jnp.split(u, 2, axis=-1)
    return x + (jax.nn.silu(gate) * val) @ w_down


def setup_inputs(seed: int = 0) -> dict:
    key = jax.random.key(seed)
    ks = jax.random.split(key, 17)
    f32 = jnp.float32
    L = DEPTH

    def nrm(k, shape, scale):
        return jax.random.normal(k, shape, f32) * scale

    def gain(k, shape):
        return 1.0 + 0.02 * jax.random.normal(k, shape, f32)

    return {
        "x": jax.random.normal(ks[0], (BATCH, SEQ, D_MODEL), f32),
        "attn_norm_w": gain(ks[1], (L, D_MODEL)),
        "w_in": nrm(ks[2], (L, D_MODEL, IN_COLS), D_MODEL ** -0.5),
        "q_norm_w": gain(ks[3], (L, HEAD_DIM)),
        "k_norm_w": gain(ks[4], (L, HEAD_DIM)),
        "sinks": nrm(ks[5], (L, N_Q_HEADS), 0.5),
        "conv_dw_w": nrm(ks[6], (L, CONV_K, CONV_CH), CONV_K ** -0.5),
        "conv_dw_b": nrm(ks[7], (L, CONV_CH), 0.02),
        "conv_ln_w": gain(ks[8], (L, CONV_CH)),
        "conv_ln_b": nrm(ks[9], (L, CONV_CH), 0.02),
        "w_out": nrm(ks[10], (L, MIX_DIM, D_MODEL), MIX_DIM ** -0.5),
        "ffn_norm_w": gain(ks[11], (L, D_MODEL)),
        "w_up": nrm(ks[12], (L, D_MODEL, 2 * D_FF), D_MODEL ** -0.5),
        "ffn_dw_w": nrm(ks[13], (L, FFN_CONV_K, 2 * D_FF), FFN_CONV_K ** -0.5),
        "ffn_dw_b": nrm(ks[14], (L, 2 * D_FF), 0.02),
        "w_down": nrm(ks[15], (L, D_FF, D_MODEL), D_FF ** -0.5),
    }


def reference(x, attn_norm_w, w_in, q_norm_w, k_norm_w, sinks, conv_dw_w, conv_dw_b,
              conv_ln_w, conv_ln_b, w_out, ffn_norm_w, w_up, ffn_dw_w, ffn_dw_b, w_down):
    for l in range(DEPTH):
        x = hybrid_layer(x, attn_norm_w[l], w_in[l], q_norm_w[l], k_norm_w[l], sinks[l],
                         conv_dw_w[l], conv_dw_b[l], conv_ln_w[l], conv_ln_b[l], w_out[l],
                         ffn_norm_w[l], w_up[l], ffn_dw_w[l], ffn_dw_b[l], w_down[l])
    return x
```

```python
import os
import numpy as np
import concourse.bass as bass
import concourse.mybir as mybir
from concourse.bass_utils import run_bass_kernel_spmd

F32 = mybir.dt.float32
BF16 = mybir.dt.bfloat16
AF = mybir.ActivationFunctionType
ALU = mybir.AluOpType

D = 2048
SEQ = 16384
NCORES = 8
TOK = 4096
HALO = 256
NH, NKV, DH = 16, 4, 64
QD, KVD = 1024, 256
CCH = 1024
CK = 31
INC = 3584
DFF = 5632
NPAIR = 44
EPS = 1e-6

IN0, OUT0, UP0, DN0 = 0, 28, 44, 132
NUNIT_SCR = 182
VW0 = 180

V_ANW, V_FNW, V_QNW, V_KNW = 0, 16, 32, 33
V_CW, V_CB, V_LNW, V_LNB = 34, 282, 290, 298
V_FW, V_FB, V_SINK = 306, 570, 658
NV = 666


class Buf:
    __slots__ = ("name", "w", "r")

    def __init__(self, name):
        self.name = name
        self.w = None
        self.r = []


class Op:
    __slots__ = ("eng", "fn", "deps", "inc", "cnt", "key")


class Sched:
    ENG = ("pe", "act", "dve", "pool", "sp")

    def __init__(self, nc):
        self.nc = nc
        self.ops = []
        self.pending = {e: set() for e in self.ENG}
        self.last = {e: None for e in self.ENG}
        self.lastkey = {}

    def add(self, eng, fn, reads=(), writes=(), key=None):
        op = Op()
        op.eng, op.fn, op.key, op.inc, op.cnt = eng, fn, key, False, 0
        idx = len(self.ops)
        deps = set()
        for b in reads:
            if b.w is not None:
                deps.add(b.w)
        for b in writes:
            if b.w is not None and not b.r:
                deps.add(b.w)
            lastr = {}
            for ri in b.r:
                p = self.ops[ri]
                if p.key is not None:
                    deps.add(ri)
                elif lastr.get(p.eng, -1) < ri:
                    lastr[p.eng] = ri
            deps.update(lastr.values())
        if self.pending[eng]:
            deps |= self.pending[eng]
            self.pending[eng] = set()
        real = set()
        for d in deps:
            p = self.ops[d]
            if eng == "pe" and p.eng == "pe" and p.key is None:
                continue
            real.add(d)
            if p.key is None:
                p.inc = True
        op.deps = real
        self.ops.append(op)
        for b in reads:
            b.r.append(idx)
        for b in writes:
            b.w = idx
            b.r = []
        if key is None:
            self.last[eng] = idx
        else:
            self.lastkey[key] = idx
        return idx

    def barrier(self, engines=("pe", "act", "dve", "pool")):
        deps = set()
        for e in ("pe", "act", "dve", "pool"):
            if self.last[e] is not None:
                deps.add(self.last[e])
                self.ops[self.last[e]].inc = True
        for k, i in self.lastkey.items():
            deps.add(i)
        for e in engines:
            self.pending[e] |= deps

    def emit(self, sems):
        nc = self.nc
        engs = {"pe": nc.tensor, "act": nc.scalar, "dve": nc.vector, "pool": nc.gpsimd, "sp": nc.sync}
        cnt = {e: 0 for e in self.ENG}
        kcnt = {}
        for op in self.ops:
            if op.key is not None:
                kcnt[op.key] = kcnt.get(op.key, 0) + 16
                op.cnt = kcnt[op.key]
            elif op.inc:
                cnt[op.eng] += 1
                op.cnt = cnt[op.eng]
        waited = {e: {} for e in self.ENG}
        nwait = 0
        for op in self.ops:
            eng = engs[op.eng]
            need = {}
            for d in op.deps:
                p = self.ops[d]
                s = ("k", p.key) if p.key is not None else ("e", p.eng)
                if need.get(s, 0) < p.cnt:
                    need[s] = p.cnt
            w = waited[op.eng]
            for s, v in need.items():
                if w.get(s, 0) < v:
                    eng.wait_ge(sems[s], v)
                    w[s] = v
                    nwait += 1
            if op.fn is None:
                continue
            ins = op.fn(eng)
            if op.key is not None:
                ins.then_inc(sems[("k", op.key)], 16)
            elif op.inc:
                ins.then_inc(sems[("e", op.eng)], 1)
        for k, v in kcnt.items():
            nc.sync.wait_ge(sems[("k", k)], v)
        return nwait


MASKMODE = os.environ.get("MK_MASK", "dve")
XQ = os.environ.get("MK_XQ", "act")


def build_program(nunits=8):
    nc = bass.Bass("TRN2", target_bir_lowering=False, dynamic_dma_scratch_size=2048)
    S = Sched(nc)

    x_ext = nc.dram_tensor("x_ext", [HALO + TOK, D], F32, kind="ExternalInput").ap()
    w_in = nc.dram_tensor("w_in", [D, INC], F32, kind="ExternalInput").ap()
    w_out = nc.dram_tensor("w_out", [D, D], F32, kind="ExternalInput").ap()
    w_up = nc.dram_tensor("w_up", [D, 2 * DFF], F32, kind="ExternalInput").ap()
    w_down = nc.dram_tensor("w_down", [DFF, D], F32, kind="ExternalInput").ap()
    vecs = nc.dram_tensor("vecs", [128, NV], F32, kind="ExternalInput").ap()
    cst = nc.dram_tensor("cst", [128, 640], F32, kind="ExternalInput").ap()
    ccore = nc.dram_tensor("ccore", [128, 129], F32, kind="ExternalInput").ap()
    y_out = nc.dram_tensor("y_out", [TOK, D], F32, kind="ExternalOutput").ap()
    wscr = nc.dram_tensor("wscr", [NUNIT_SCR, 128, 2048], BF16, kind="Internal").ap()

    off = [2048]

    def sb(name, shape, dt, at=None):
        nbytes = int(np.prod(shape[1:])) * (4 if dt == F32 else 2)
        if at is None:
            o = off[0]
            off[0] += (nbytes + 63) // 64 * 64
        else:
            o = at
        t = nc.alloc_sbuf_tensor_at(name, list(shape), dt, offset=o)
        return t, o, nbytes

    ident, _, _ = sb("ident", [128, 128], F32)
    ones_bf, _, _ = sb("ones_bf", [128, 128], BF16)
    blk_bf, _, _ = sb("blk_bf", [128, 128], BF16)
    maskC, _, _ = sb("maskC", [128, 4, 128], BF16)
    maskP, _, _ = sb("maskP", [128, 4, 128], BF16)
    maskP0, _, _ = sb("maskP0", [128, 4, 128], BF16)
    es = [sb(f"es{i}", [128, 4, 128], F32)[0] for i in range(2)]
    vec, _, _ = sb("vec", [128, NV], F32)
    hmask, _, _ = sb("hmask", [128, 1], F32)
    qsc, _, _ = sb("qsc", [128, 1], F32)
    essm, _, _ = sb("essm", [128, 8], F32)
    zer, _, _ = sb("zer", [128, 128], F32)
    xT, _, _ = sb("xT", [128, 16, 512], F32)
    hT, _, _ = sb("hT", [128, 16, 512], BF16)
    knT, _, _ = sb("knT", [128, 2, 640], BF16)
    Vr, _, _ = sb("Vr", [128, 5, 256], BF16)
    h2halo, _, _ = sb("h2halo", [128, 16, 2], BF16)
    carry = [sb(f"carry{i}", [128, 88, 2], F32)[0] for i in range(2)]
    ccarry, _, _ = sb("ccarry", [128, 8, 30], F32)
    NRING = 6
    wring = [sb(f"wring{i}", [128, 2048], BF16)[0] for i in range(NRING)]
    xs = [sb(f"xs{i}", [128, 2048], F32)[0] for i in range(2)]
    ident_bf, _, _ = sb("ident_bf", [128, 128], BF16)
    A0 = off[0]
    sq, _, _ = sb("sq", [128, 16, 512], BF16)
    stt = [sb(f"stt{i}", [128, 512], F32)[0] for i in range(4)]
    qnT, _, _ = sb("qnT", [128, 8, 512], BF16)
    sg = [sb(f"sg{i}", [128, 512], F32)[0] for i in range(2)]
    cbuf, _, _ = sb("cbuf", [128, 8, 544], F32)
    ybuf, _, _ = sb("ybuf", [128, 8, 512], F32)
    NE = 8
    Et = [sb(f"E{i}", [128, 4, 128], BF16)[0] for i in range(NE)]
    dtm = [sb(f"dtm{i}", [128, 512], F32)[0] for i in range(3)]
    A_END_M = off[0]
    off[0] = A0
    yb = [sb(f"yb{i}", [128, 512], F32)[0] for i in range(4)]
    sbf = [sb(f"sbf{i}", [128, 512], F32)[0] for i in range(2)]
    act, _, _ = sb("act", [128, NPAIR, 512], BF16)
    oT = [sb(f"oT{i}", [128, 512], F32)[0] for i in range(4)]
    ostage = [sb(f"ostage{i}", [128, 512], F32)[0] for i in range(3)]
    cvs = [sb(f"cvs{i}", [128, 1024], F32)[0] for i in range(3)]
    A_END_F = off[0]
    off[0] = A0
    pstage = [sb(f"pstage{i}", [128, 2048], F32)[0] for i in range(3)]
    pout = [sb(f"pout{i}", [128, 2048], BF16)[0] for i in range(3)]
    cstage, _, _ = sb("cstage", [128, 640], F32)
    ccstage, _, _ = sb("ccstage", [128, 129], F32)
    A_END_P = off[0]
    total = max(A_END_M, A_END_F, A_END_P)
    assert total <= 192 * 1024, total
    if os.environ.get("MK_VERBOSE"):
        print("SBUF: A0", A0, "mixer", A_END_M - A0, "ffn", A_END_F - A0, "pre", A_END_P - A0, "total", total)

    banks = [nc.alloc_psum_tensor(f"pb{i}", [128, 512], F32) for i in range(8)]
    bankb = [Buf(f"pb{i}") for i in range(8)]
    rr = {"main": 0, "all": 0}

    def next_bank(pool="main"):
        nb_ = 6 if pool == "main" else 7
        i = rr[pool] % nb_
        rr[pool] += 1
        return i

    xTb = [Buf(f"xT{k}") for k in range(16)]
    hTb = [Buf(f"hT{k}") for k in range(16)]
    sqb = [Buf(f"sq{k}") for k in range(16)]
    xsb = [Buf(f"xs{i}") for i in range(2)]
    kn_c = Buf("kn_c")
    kn_m = [Buf(f"kn_m{i}") for i in range(2)]
    vslot = [Buf(f"vs{i}") for i in range(5)]
    qnb = [Buf(f"qn{i}") for i in range(8)]
    cb_c = Buf("cb_c")
    ccb = Buf("ccarry")
    cb_m = [Buf(f"cb_m{i}") for i in range(8)]
    ybb = [Buf(f"ybuf{i}") for i in range(8)]
    sttb = [Buf(f"stt{i}") for i in range(4)]
    sgb = [Buf(f"sg{i}") for i in range(2)]
    Eb = [Buf(f"E{i}") for i in range(NE)]
    dtb = [Buf(f"dt{i}") for i in range(3)]
    h2hb = Buf("h2halo")
    carb = [[Buf(f"car{p}_{c}") for c in range(88)] for p in range(2)]
    ringb = [Buf(f"ring{i}") for i in range(NRING)]
    wvb = Buf("wv")
    ybf = [Buf(f"yb{i}") for i in range(4)]
    sbb = [Buf(f"sb{i}") for i in range(2)]
    actb = [Buf(f"act{j}") for j in range(NPAIR)]
    oTb = [Buf(f"oT{i}") for i in range(4)]
    ostb = [Buf(f"ost{i}") for i in range(3)]
    pstp = [[Buf(f"pst{i}_{l}") for l in range(6)] for i in range(3)]
    poutb = [Buf(f"pout{i}") for i in range(3)]
    cvsb = [Buf(f"cvs{i}") for i in range(3)]
    ctr = {"cvs": 0, "ring": 0, "stt": 0, "sg": 0, "E": 0, "dt": 0, "yb": 0, "sb": 0, "ost": 0, "pp": 0, "xs": 0}

    def rot(name, n):
        i = ctr[name] % n
        ctr[name] += 1
        return i

    cb_ = Buf("cstage")
    S.add("sp", lambda e: e.dma_start(out=cstage[:], in_=cst[:, :]), writes=[cb_], key="const0")
    ccb_ = Buf("ccstage")
    S.add("sp", lambda e: e.dma_start(out=ccstage[:], in_=ccore[:, :]), writes=[ccb_], key="const1")
    vecb = Buf("vec")
    S.add("sp", lambda e: e.dma_start(out=vec[:], in_=vecs[:, :]), writes=[vecb], key="const2")
    cbuf_ = Buf("consts")
    S.add("dve", lambda e: e.tensor_copy(out=ident[:], in_=cstage[:, 0:128]), reads=[cb_], writes=[cbuf_])
    S.add("dve", lambda e: e.tensor_copy(out=ones_bf[:], in_=cstage[:, 128:256]), reads=[cb_], writes=[cbuf_])
    S.add("dve", lambda e: e.tensor_copy(out=ident_bf[:], in_=cstage[:, 0:128]), reads=[cb_], writes=[cbuf_])
    S.add("dve", lambda e: e.tensor_copy(out=blk_bf[:], in_=cstage[:, 256:384]), reads=[cb_], writes=[cbuf_])
    for j in range(4):
        S.add("dve", lambda e, j=j: e.tensor_copy(out=maskC[:, j, :], in_=cstage[:, 384:512]), reads=[cb_], writes=[cbuf_])
        S.add("dve", lambda e, j=j: e.tensor_copy(out=maskP[:, j, :], in_=cstage[:, 512:640]), reads=[cb_], writes=[cbuf_])
        S.add("dve", lambda e, j=j: e.tensor_copy(out=maskP0[:, j, :], in_=ccstage[:, 0:128]), reads=[ccb_], writes=[cbuf_])
    S.add("dve", lambda e: e.tensor_copy(out=hmask[:], in_=ccstage[:, 128:129]), reads=[ccb_], writes=[cbuf_])
    S.add("dve", lambda e: e.memset(zer[:], 0.0), writes=[cbuf_])
    S.add("act", lambda e: e.activation(out=qsc[:], in_=vec[:, V_QNW:V_QNW + 1], func=AF.Identity, scale=0.125), reads=[vecb], writes=[cbuf_])
    S.add("act", lambda e: e.activation(out=essm[:], in_=vec[:, V_SINK:V_SINK + 8], func=AF.Exp), reads=[vecb], writes=[cbuf_])
    for pp in range(2):
        for j in range(4):
            S.add("dve", lambda e, pp=pp, j=j: e.tensor_scalar(
                out=es[pp][:, j, :], in0=zer[:], scalar1=essm[:, pp * 4 + j:pp * 4 + j + 1], scalar2=None, op0=ALU.add),
                reads=[cbuf_], writes=[cbuf_])
    S.add("pool", lambda e: e.memset(knT[:, :, 0:128], 0.0), writes=[kn_c])
    S.add("pool", lambda e: e.memset(Vr[:, 0, :], 0.0), writes=[vslot[0]])

    win_v = w_in.rearrange("(k p) c -> p k c", p=128)
    wup_v = w_up.rearrange("(k p) c -> p k c", p=128)
    wdn_v = w_down.rearrange("(k p) c -> p k c", p=128)
    woc_v = w_out[1024:2048, :].rearrange("(k p) c -> p k c", p=128)
    woa_v = w_out[0:1024, :].rearrange("(pp e j r) c -> e r pp j c", pp=2, e=2, j=4, r=64)
    cast_eng = ["act", "dve", "act", "dve", "pool"]

    def prepass_unit(loads, nk, dst):
        i = rot("pp", 3)
        for li, ld in enumerate(loads):
            S.add("sp", lambda e, ld=ld, i=i: ld(e, pstage[i]), writes=[pstp[i][li]], key=f"pin{i}")
        rd = pstp[i][0:len(loads)]
        ce = cast_eng[ctr["pp"] % len(cast_eng)]
        if ce == "act":
            S.add("act", lambda e, i=i: e.copy(out=pout[i][:, 0:nk * 128], in_=pstage[i][:, 0:nk * 128]),
                  reads=rd, writes=[poutb[i]])
        else:
            S.add(ce, lambda e, i=i: e.tensor_copy(out=pout[i][:, 0:nk * 128], in_=pstage[i][:, 0:nk * 128]),
                  reads=rd, writes=[poutb[i]])
        S.add("act", lambda e, i=i: e.dma_start(out=dst, in_=pout[i][:, 0:nk * 128]), reads=[poutb[i]], key=f"pout{i}")

    def st3(t, nk=16):
        return t[:, 0:nk * 128].rearrange("p (k c) -> p k c", c=128)

    in_order = []
    for i in range(8):
        in_order += [("a", i), ("g", i)]
    in_order += [("k", 0), ("k", 1)] + [("q", c) for c in range(8)]
    for s, (kind, i) in enumerate(in_order):
        if kind == "q":
            pp, j = divmod(i, 4)
            lds = []
            for e_ in range(2):
                head = 8 * pp + 4 * e_ + j
                lds.append(lambda e, t, head=head, e_=e_: e.dma_start(
                    out=st3(t)[:, :, e_ * 64:(e_ + 1) * 64], in_=win_v[:, :, head * 64:(head + 1) * 64]))
        else:
            c0_ = {"a": 1536, "g": 2560, "k": 1024}[kind] + i * 128
            lds = [lambda e, t, c0_=c0_: e.dma_start(out=st3(t), in_=win_v[:, :, c0_:c0_ + 128])]
        prepass_unit(lds, 16, wscr[IN0 + s])
    for h in range(2):
        lds = [lambda e, t, h=h: e.dma_start(out=t[:, :].rearrange("p (k c) -> p k c", c=256),
                                              in_=win_v[:, h * 8:(h + 1) * 8, 1280:1536])]
        prepass_unit(lds, 16, wscr[VW0 + h])
    for m in range(16):
        lds = []
        for e_ in range(2):
            for pp_ in range(2):
                lds.append(lambda e, t, m=m, e_=e_, pp_=pp_: e.dma_start(
                    out=st3(t)[e_ * 64:(e_ + 1) * 64, 4 * pp_:4 * pp_ + 4, :],
                    in_=woa_v[e_, :, pp_, :, m * 128:(m + 1) * 128]))
        lds.append(lambda e, t, m=m: e.dma_start(out=st3(t)[:, 8:16, :], in_=woc_v[:, :, m * 128:(m + 1) * 128]))
        prepass_unit(lds, 16, wscr[OUT0 + m])
    S.barrier(engines=("pe", "act", "dve", "pool", "sp"))

    def load_unit(scr_idx, ncols=2048):
        i = rot("ring", NRING)
        S.add("sp", lambda e, i=i: e.dma_start(out=wring[i][:, 0:ncols], in_=wscr[scr_idx][:, 0:ncols]),
              writes=[ringb[i]], key=f"ring{i}")
        return wring[i], ringb[i]

    def load_unit_cvt(scr_idx, nk, src):
        i = rot("ring", NRING)
        for h in range(2):
            k0, k1 = 8 * h, min(nk, 8 * h + 8)
            if k1 <= k0:
                continue
            si = rot("cvs", 3)
            if src[0] == "up":
                ap = wup_v[:, k0:k1, src[1] * 128:(src[1] + 1) * 128]
            else:
                ap = wdn_v[:, 16 * src[2] + k0:16 * src[2] + k1, src[1] * 128:(src[1] + 1) * 128]
            S.add("sp", lambda e, si=si, ap=ap, k0=k0, k1=k1: e.dma_start(
                out=cvs[si][:, 0:(k1 - k0) * 128].rearrange("p (k c) -> p k c", c=128), in_=ap),
                writes=[cvsb[si]], key=f"cvs{si}")
            S.add("act", lambda e, si=si, i=i, k0=k0, k1=k1: e.copy(
                out=wring[i][:, k0 * 128:k1 * 128], in_=cvs[si][:, 0:(k1 - k0) * 128]),
                reads=[cvsb[si]], writes=[ringb[i]])
        S.add("act", lambda e, i=i: e.dma_start(out=wscr[scr_idx][:, 0:nk * 128], in_=wring[i][:, 0:nk * 128]),
              reads=[ringb[i]], key=f"rst{i}")
        return wring[i], ringb[i]

    def V(c, n=1):
        return vec[:, c:c + n]

    conv_state = {"glu": 0, "gen": None}

    xslots = {}
    head_done = {}

    def unit_geo(u):
        n = 256 if u < 0 else 512
        r0 = 0 if u < 0 else HALO + 512 * u
        return n, n // 128, r0

    def x_load(u, g):
        n, nb, r0 = unit_geo(u)
        xi = rot("xs", 2)
        S.add(XQ, lambda e: e.dma_start(
            out=xs[xi][:, 0:nb * 512].rearrange("p (b c) -> p b c", c=512),
            in_=x_ext[r0:r0 + n, 512 * g:512 * g + 512].rearrange("(b p) c -> p b c", p=128)),
            writes=[xsb[xi]], key=f"xs{xi}")
        xslots[(u, g)] = xi

    def head_group(u, g, pool):
        n, nb, r0 = unit_geo(u)
        xi = xslots.pop((u, g))
        for kk in range(4):
            k = 4 * g + kk
            bk = next_bank(pool)
            for b in range(nb):
                S.add("pe", lambda e, bk=bk, kk=kk, b=b: e.transpose(
                    out=banks[bk][:, b * 128:(b + 1) * 128], in_=xs[xi][:, b * 512 + kk * 128:b * 512 + (kk + 1) * 128],
                    identity=ident[:]), reads=[xsb[xi]], writes=[bankb[bk]])
            S.add("act", lambda e, bk=bk, k=k: e.copy(out=xT[:, k, 0:n], in_=banks[bk][:, 0:n]),
                  reads=[bankb[bk]], writes=[xTb[k]])
            S.add("act", lambda e, bk=bk, k=k: e.activation(out=sq[:, k, 0:n], in_=banks[bk][:, 0:n], func=AF.Square),
                  reads=[bankb[bk]], writes=[sqb[k]])

    def head_finish(u, pool):
        n, nb, r0 = unit_geo(u)
        bs = next_bank(pool)
        for k in range(16):
            S.add("pe", lambda e, k=k: e.matmul(banks[bs][:, 0:n], lhsT=ones_bf[:], rhs=sq[:, k, 0:n],
                                                start=(k == 0), stop=(k == 15)),
                  reads=[sqb[k]], writes=[bankb[bs]])
        t1 = rot("stt", 4)
        S.add("act", lambda e: e.activation(out=stt[t1][:, 0:n], in_=banks[bs][:, 0:n], func=AF.Ln,
                                            scale=1.0 / D, bias=EPS), reads=[bankb[bs]], writes=[sttb[t1]])
        S.add("act", lambda e: e.activation(out=banks[bs][:, 0:n], in_=stt[t1][:, 0:n], func=AF.Exp, scale=-0.5),
              reads=[sttb[t1]], writes=[bankb[bs]])
        for k in range(16):
            S.add("dve", lambda e, k=k: e.scalar_tensor_tensor(
                out=hT[:, k, 0:n], in0=xT[:, k, 0:n], scalar=V(V_ANW + k), in1=banks[bs][:, 0:n],
                op0=ALU.mult, op1=ALU.mult), reads=[xTb[k], bankb[bs]], writes=[hTb[k]])
        head_done[u] = True

    def head_inline(u):
        x_load(u, 0)
        x_load(u, 1)
        for g in range(4):
            head_group(u, g, "main")
            if g + 2 < 4:
                x_load(u, g + 2)
        head_finish(u, "main")

    def mixer(u):
        n = 256 if u < 0 else 512
        c0 = 128 if u < 0 else 0
        w = n - c0
        nb = n // 128
        r0 = 0 if u < 0 else HALO + 512 * u
        if u >= 0:
            pn = 256 if u == 0 else 512
            pnb = pn // 128
            S.add("pool", lambda e: e.tensor_copy(out=knT[:, :, 0:128], in_=knT[:, :, pn:pn + 128]),
                  reads=kn_m, writes=[kn_c])
            S.add("pool", lambda e: e.tensor_copy(out=Vr[:, 0, :], in_=Vr[:, pnb, :]), reads=[vslot[pnb]], writes=[vslot[0]])
            S.add("pool", lambda e: e.tensor_copy(out=cbuf[:, :, 0:30], in_=ccarry[:, :, :]), reads=[ccb], writes=[cb_c])
        else:
            S.add("pool", lambda e: e.memset(cbuf[:, :, 0:30], 0.0), writes=[cb_c])
        if not head_done.get(u):
            head_inline(u)

        NPT = int(os.environ.get("MK_NPT", "10"))

        def conv_gen():
            for i in range(8):
                while conv_state["glu"] <= i:
                    yield False
                cvb = 6 + (i % 2)
                for k in range(NPT):
                    if k == 0:
                        S.add("pool", lambda e, i=i, k=k: e.tensor_scalar(
                            out=ybuf[:, i, c0:n], in0=cbuf[:, i, c0 + k:n + k], scalar1=V(V_CW + i * 31 + k), scalar2=0.0,
                            op0=ALU.mult, op1=ALU.add), reads=[cb_m[i], cb_c], writes=[ybb[i]])
                    else:
                        S.add("pool", lambda e, i=i, k=k: e.tensor_scalar(
                            out=dtm[2][:, 0:w], in0=cbuf[:, i, c0 + k:n + k], scalar1=V(V_CW + i * 31 + k), scalar2=0.0,
                            op0=ALU.mult, op1=ALU.add), reads=[cb_m[i], cb_c], writes=[dtb[2]])
                        S.add("pool", lambda e, i=i: e.tensor_tensor(
                            out=ybuf[:, i, c0:n], in0=ybuf[:, i, c0:n], in1=dtm[2][:, 0:w], op=ALU.add),
                            reads=[dtb[2], ybb[i]], writes=[ybb[i]])
                S.add("act", lambda e, i=i, cvb=cvb: e.activation(
                    out=banks[cvb][:, 0:w], in_=cbuf[:, i, c0 + 30:n + 30], func=AF.Identity,
                    scale=V(V_CW + i * 31 + 30), bias=V(V_CB + i)),
                    reads=[cb_m[i]], writes=[bankb[cvb]])
                yield True
                for k in range(29, NPT - 1, -1):
                    S.add("dve", lambda e, i=i, k=k, cvb=cvb: e.scalar_tensor_tensor(
                        out=banks[cvb][:, 0:w], in0=cbuf[:, i, c0 + k:n + k], scalar=V(V_CW + i * 31 + k),
                        in1=banks[cvb][:, 0:w], op0=ALU.mult, op1=ALU.add),
                        reads=[cb_m[i], cb_c], writes=[bankb[cvb]])
                    yield True
                def fin(i=i, cvb=cvb):
                    S.add("dve", lambda e: e.tensor_tensor(
                        out=ybuf[:, i, c0:n], in0=banks[cvb][:, 0:w], in1=ybuf[:, i, c0:n], op=ALU.add),
                        reads=[bankb[cvb], ybb[i]], writes=[ybb[i]])
                    S.add("act", lambda e: e.copy(out=sq[:, i, c0:n], in_=ybuf[:, i, c0:n]),
                          reads=[ybb[i]], writes=[sqb[i]])
                    S.add("act", lambda e: e.activation(out=sq[:, 8 + i, c0:n], in_=ybuf[:, i, c0:n], func=AF.Square),
                          reads=[ybb[i]], writes=[sqb[8 + i]])
                if pend_fin:
                    pend_fin.pop()()
                pend_fin.append(fin)
                yield True
            while pend_fin:
                pend_fin.pop()()
            yield True

        pend_fin = []
        conv_state["glu"] = 0
        cg = conv_gen()
        cdone = [False]

        def conv_adv(k):
            if cdone[0]:
                return
            for _ in range(k):
                try:
                    r = next(cg)
                except StopIteration:
                    cdone[0] = True
                    return
                if r is False:
                    return

        deferred = []
        pend_a = {}
        for s, (kind, i) in enumerate(in_order):
            slot, slb = load_unit(IN0 + s)
            bk = next_bank()
            for k in range(16):
                S.add("pe", lambda e, bk=bk, k=k, slot=slot: e.matmul(
                    banks[bk][:, 0:n], lhsT=slot[:, k * 128:(k + 1) * 128], rhs=hT[:, k, 0:n],
                    start=(k == 0), stop=(k == 15)), reads=[slb, hTb[k]], writes=[bankb[bk]])
            for f in deferred:
                f()
            deferred = []
            if kind == "a":
                pend_a[i] = bk
            elif kind == "g":
                ba = pend_a.pop(i)
                sgi = rot("sg", 2)
                S.add("act", lambda e, bk=bk, sgi=sgi: e.activation(out=sg[sgi][:, 0:n], in_=banks[bk][:, 0:n],
                                                                   func=AF.Sigmoid),
                      reads=[bankb[bk]], writes=[sgb[sgi]])
                S.add("dve", lambda e, ba=ba, sgi=sgi, i=i: e.tensor_tensor(
                    out=cbuf[:, i, 30:30 + n], in0=banks[ba][:, 0:n], in1=sg[sgi][:, 0:n], op=ALU.mult),
                    reads=[bankb[ba], sgb[sgi]], writes=[cb_m[i]])
                conv_state["glu"] = i + 1
            else:
                ei = rot("E", NE)
                sqq = Et[ei][:, :, :].rearrange("p a c -> p (a c)")
                S.add("act", lambda e, bk=bk, sqq=sqq: e.activation(out=sqq[:, 0:n], in_=banks[bk][:, 0:n], func=AF.Square),
                      reads=[bankb[bk]], writes=[Eb[ei]])

                def post(bk=bk, ei=ei, sqq=sqq, kind=kind, i=i):
                    b2 = next_bank()
                    S.add("pe", lambda e: e.matmul(banks[b2][:, 0:n], lhsT=blk_bf[:], rhs=sqq[:, 0:n], start=True, stop=True),
                          reads=[Eb[ei]], writes=[bankb[b2]])
                    t2 = rot("stt", 4)
                    S.add("act", lambda e: e.activation(out=stt[t2][:, 0:n], in_=banks[b2][:, 0:n], func=AF.Ln,
                                                        scale=1.0 / DH, bias=EPS), reads=[bankb[b2]], writes=[sttb[t2]])
                    S.add("act", lambda e: e.activation(out=stt[t2][:, 0:n], in_=stt[t2][:, 0:n], func=AF.Exp, scale=-0.5),
                          reads=[sttb[t2]], writes=[sttb[t2]])
                    if kind == "q":
                        S.add("dve", lambda e: e.scalar_tensor_tensor(
                            out=qnT[:, i, 0:n], in0=banks[bk][:, 0:n], scalar=qsc[:, 0:1], in1=stt[t2][:, 0:n],
                            op0=ALU.mult, op1=ALU.mult), reads=[bankb[bk], sttb[t2]], writes=[qnb[i]])
                    else:
                        S.add("dve", lambda e: e.scalar_tensor_tensor(
                            out=knT[:, i, 128:128 + n], in0=banks[bk][:, 0:n], scalar=V(V_KNW), in1=stt[t2][:, 0:n],
                            op0=ALU.mult, op1=ALU.mult), reads=[bankb[bk], sttb[t2]], writes=[kn_m[i]])
                deferred.append(post)
            conv_adv(6)
        vsl = [load_unit(VW0 + h_) for h_ in range(2)]
        for b in range(nb):
            bk = next_bank()
            for k in range(16):
                h_, kk_ = divmod(k, 8)
                S.add("pe", lambda e, bk=bk, k=k, b=b, h_=h_, kk_=kk_: e.matmul(
                    banks[bk][:, 0:256], lhsT=hT[:, k, b * 128:(b + 1) * 128],
                    rhs=vsl[h_][0][:, kk_ * 256:(kk_ + 1) * 256], start=(k == 0), stop=(k == 15)),
                    reads=[vsl[h_][1], hTb[k]], writes=[bankb[bk]])
            if b == 0:
                for f in deferred:
                    f()
                deferred = []
            S.add("act", lambda e, bk=bk, b=b: e.copy(out=Vr[:, b + 1, :], in_=banks[bk][:, 0:256]),
                  reads=[bankb[bk]], writes=[vslot[b + 1]])
            conv_adv(4)

        S.add("pool", lambda e: e.tensor_copy(out=ccarry[:, :, :], in_=cbuf[:, :, n:n + 30]), reads=cb_m, writes=[ccb])
        def att_phase1(qb, pp):
            items = []
            for e_ in range(2):
                g = 2 * pp + e_
                for kbi in range(2):
                    kcol = qb * 128 + kbi * 128
                    vs_ = qb + kbi
                    if kbi == 0:
                        mk = maskP0 if (u == 0 and qb == 0) else maskP
                        knr = [kn_c] if qb == 0 else [kn_m[pp]]
                    else:
                        mk = maskC
                        knr = [kn_m[pp]]
                    sbk = next_bank()
                    ei = rot("E", NE)
                    S.add("pe", lambda e, sbk=sbk, e_=e_, pp=pp, kcol=kcol, qb=qb: e.matmul(
                        banks[sbk][:, :].rearrange("p (a c) -> p a c", c=128),
                        lhsT=knT[e_ * 64:(e_ + 1) * 64, pp, kcol:kcol + 128],
                        rhs=qnT[e_ * 64:(e_ + 1) * 64, 4 * pp:4 * pp + 4, qb * 128:(qb + 1) * 128],
                        start=True, stop=(MASKMODE != "pe")),
                        reads=knr + [qnb[4 * pp + j] for j in range(4)], writes=[bankb[sbk]])
                    if MASKMODE == "pe":
                        S.add("pe", lambda e, sbk=sbk, mk=mk: e.matmul(
                            banks[sbk][:, :].rearrange("p (a c) -> p a c", c=128), lhsT=ident_bf[:], rhs=mk[:, :, :],
                            start=False, stop=True), reads=[], writes=[bankb[sbk]])
                    S.add("act", lambda e, sbk=sbk, ei=ei: e.activation(
                        out=Et[ei][:, :, :], in_=banks[sbk][:, :].rearrange("p (a c) -> p a c", c=128), func=AF.Exp),
                        reads=[bankb[sbk]], writes=[Eb[ei]])
                    if MASKMODE != "pe":
                        S.add("pool" if MASKMODE == "pool" else "dve", lambda e, ei=ei, mk=mk: e.tensor_tensor(
                            out=Et[ei][:, :, :], in0=Et[ei][:, :, :], in1=mk[:, :, :], op=ALU.mult),
                            reads=[Eb[ei]], writes=[Eb[ei]])
                    items.append((e_, g, kbi, vs_, ei))
            return items

        def att_phase2(qb, pp, items):
            pvb = next_bank()
            smb = next_bank()
            for (e_, g, kbi, vs_, ei) in items:
                S.add("pe", lambda e, pvb=pvb, e_=e_, g=g, vs_=vs_, ei=ei, kbi=kbi: e.matmul(
                    banks[pvb][e_ * 64:(e_ + 1) * 64, :], lhsT=Vr[:, vs_, g * 64:(g + 1) * 64],
                    rhs=Et[ei][:, :, :].rearrange("p a c -> p (a c)"), start=(kbi == 0), stop=(kbi == 1)),
                    reads=[vslot[vs_], Eb[ei]], writes=[bankb[pvb]])
                S.add("pe", lambda e, smb=smb, e_=e_, ei=ei, kbi=kbi: e.matmul(
                    banks[smb][e_ * 64:(e_ + 1) * 64, :], lhsT=ones_bf[:, 0:64],
                    rhs=Et[ei][:, :, :].rearrange("p a c -> p (a c)"), start=(kbi == 0), stop=(kbi == 1)),
                    reads=[Eb[ei]], writes=[bankb[smb]])
            di = rot("dt", 2)
            S.add("dve", lambda e, smb=smb, di=di, pp=pp: e.tensor_tensor(
                out=dtm[di][:, :], in0=banks[smb][:, :], in1=es[pp][:, :, :].rearrange("p a c -> p (a c)"), op=ALU.add),
                reads=[bankb[smb]], writes=[dtb[di]])
            S.add("act", lambda e, di=di: e.activation(out=dtm[di][:, :], in_=dtm[di][:, :], func=AF.Ln),
                  reads=[dtb[di]], writes=[dtb[di]])
            S.add("act", lambda e, di=di: e.activation(out=dtm[di][:, :], in_=dtm[di][:, :], func=AF.Exp, scale=-1.0),
                  reads=[dtb[di]], writes=[dtb[di]])
            S.add("dve", lambda e, pvb=pvb, di=di, pp=pp, qb=qb: e.tensor_tensor(
                out=hT[:, 4 * pp:4 * pp + 4, qb * 128:(qb + 1) * 128],
                in0=banks[pvb][:, :].rearrange("p (a c) -> p a c", c=128),
                in1=dtm[di][:, :].rearrange("p (a c) -> p a c", c=128), op=ALU.mult),
                reads=[bankb[pvb], dtb[di]], writes=[hTb[4 * pp + j] for j in range(4)])
            conv_adv(8)

        prev_step = None
        for qb in range(c0 // 128, nb):
            for pp in range(2):
                items = att_phase1(qb, pp)
                if prev_step is not None:
                    att_phase2(*prev_step)
                prev_step = (qb, pp, items)
        att_phase2(*prev_step)
        while not cdone[0]:
            conv_adv(1000)

        bM = next_bank()
        bQ = next_bank()
        for i in range(8):
            S.add("pe", lambda e, i=i: e.matmul(banks[bM][:, 0:w], lhsT=ones_bf[:], rhs=sq[:, i, c0:n],
                                                start=(i == 0), stop=(i == 7)), reads=[sqb[i]], writes=[bankb[bM]])
        for i in range(8):
            S.add("pe", lambda e, i=i: e.matmul(banks[bQ][:, 0:w], lhsT=ones_bf[:], rhs=sq[:, 8 + i, c0:n],
                                                start=(i == 0), stop=(i == 7)), reads=[sqb[8 + i]], writes=[bankb[bQ]])
        t3 = rot("stt", 4)
        S.add("act", lambda e: e.activation(out=stt[t3][:, 0:w], in_=banks[bM][:, 0:w], func=AF.Square, scale=1.0 / CCH),
              reads=[bankb[bM]], writes=[sttb[t3]])
        S.add("dve", lambda e: e.scalar_tensor_tensor(
            out=stt[t3][:, 0:w], in0=banks[bQ][:, 0:w], scalar=1.0 / CCH, in1=stt[t3][:, 0:w],
            op0=ALU.mult, op1=ALU.subtract), reads=[bankb[bQ], sttb[t3]], writes=[sttb[t3]])
        S.add("act", lambda e: e.activation(out=stt[t3][:, 0:w], in_=stt[t3][:, 0:w], func=AF.Ln, bias=EPS),
              reads=[sttb[t3]], writes=[sttb[t3]])
        S.add("act", lambda e: e.activation(out=banks[bQ][:, 0:w], in_=stt[t3][:, 0:w], func=AF.Exp, scale=-0.5),
              reads=[sttb[t3]], writes=[bankb[bQ]])
        for i in range(8):
            d1 = rot("dt", 2)
            S.add("dve", lambda e, i=i, d1=d1: e.scalar_tensor_tensor(
                out=dtm[d1][:, 0:w], in0=banks[bM][:, 0:w], scalar=-1.0 / CCH, in1=ybuf[:, i, c0:n],
                op0=ALU.mult, op1=ALU.add), reads=[bankb[bM], ybb[i]], writes=[dtb[d1]])
            S.add("dve", lambda e, d1=d1: e.tensor_tensor(
                out=dtm[d1][:, 0:w], in0=dtm[d1][:, 0:w], in1=banks[bQ][:, 0:w], op=ALU.mult),
                reads=[dtb[d1], bankb[bQ]], writes=[dtb[d1]])
            S.add("act", lambda e, i=i, d1=d1: e.activation(
                out=hT[:, 8 + i, c0:n], in_=dtm[d1][:, 0:w], func=AF.Silu, scale=V(V_LNW + i), bias=V(V_LNB + i)),
                reads=[dtb[d1]], writes=[hTb[8 + i]])

        for m in range(16):
            slot, slb = load_unit(OUT0 + m)
            bk = next_bank()
            for k in range(16):
                S.add("pe", lambda e, bk=bk, k=k, slot=slot: e.matmul(
                    banks[bk][:, 0:w], lhsT=slot[:, k * 128:(k + 1) * 128], rhs=hT[:, k, c0:n],
                    start=(k == 0), stop=(k == 15)), reads=[slb, hTb[k]], writes=[bankb[bk]])
            S.add("dve", lambda e, bk=bk, m=m: e.tensor_tensor(
                out=xT[:, m, c0:n], in0=banks[bk][:, 0:w], in1=xT[:, m, c0:n], op=ALU.add),
                reads=[bankb[bk], xTb[m]], writes=[xTb[m]])
            S.add("act", lambda e, m=m: e.activation(out=sq[:, m, c0:n], in_=xT[:, m, c0:n], func=AF.Square),
                  reads=[xTb[m]], writes=[sqb[m]])
        bs2 = next_bank()
        for m in range(16):
            S.add("pe", lambda e, m=m: e.matmul(banks[bs2][:, 0:w], lhsT=ones_bf[:], rhs=sq[:, m, c0:n],
                                                start=(m == 0), stop=(m == 15)), reads=[sqb[m]], writes=[bankb[bs2]])
        t4 = rot("stt", 4)
        S.add("act", lambda e: e.activation(out=stt[t4][:, 0:w], in_=banks[bs2][:, 0:w], func=AF.Ln,
                                            scale=1.0 / D, bias=EPS), reads=[bankb[bs2]], writes=[sttb[t4]])
        S.add("act", lambda e: e.activation(out=banks[bs2][:, 0:w], in_=stt[t4][:, 0:w], func=AF.Exp, scale=-0.5),
              reads=[sttb[t4]], writes=[bankb[bs2]])
        for m in range(16):
            S.add("dve", lambda e, m=m: e.scalar_tensor_tensor(
                out=hT[:, m, c0:n], in0=xT[:, m, c0:n], scalar=V(V_FNW + m), in1=banks[bs2][:, 0:w],
                op0=ALU.mult, op1=ALU.mult), reads=[xTb[m], bankb[bs2]], writes=[hTb[m]])
        if u < 0:
            S.add("pool", lambda e: e.tensor_copy(out=h2halo[:, :, :], in_=hT[:, :, n - 2:n]), reads=hTb, writes=[h2hb])
            pass

    HB = 7

    def ffn(u):
        par = u % 2
        order = []
        for j in range(NPAIR):
            for cc in (j, NPAIR + j):
                order.append((UP0 + cc, 16, ("up", cc)))
        for m in range(16):
            for t_ in range(3):
                order.append((DN0 + 3 * m + t_, 16 if t_ < 2 else 12, ("dn", m, t_)))
        slots = {}
        nxt = [0]
        look = 3 if u == 0 else 0

        def get_slab(si_):
            while nxt[0] <= min(si_ + look, len(order) - 1):
                scr, nk_, src = order[nxt[0]]
                slots[nxt[0]] = load_unit_cvt(scr, nk_, src) if u == 0 else load_unit(scr, nk_ * 128)
                nxt[0] += 1
            return slots.pop(si_)

        sidx = 0
        for j in range(NPAIR):
            yis = []
            for cc in (j, NPAIR + j):
                slot, slb = get_slab(sidx)
                sidx += 1
                bk = next_bank("all")
                for k in range(16):
                    S.add("pe", lambda e, bk=bk, k=k, slot=slot: e.matmul(
                        banks[bk][:, :], lhsT=slot[:, k * 128:(k + 1) * 128], rhs=hT[:, k, :],
                        start=(k == 0), stop=(k == 15)), reads=[slb, hTb[k]], writes=[bankb[bk]])
                if u == 0:
                    for k in range(16):
                        S.add("pe", lambda e, k=k, slot=slot, cc=cc: e.matmul(
                            banks[HB][:, 2 * cc:2 * cc + 2], lhsT=slot[:, k * 128:(k + 1) * 128], rhs=h2halo[:, k, :],
                            start=(k == 0), stop=(k == 15)), reads=[slb, h2hb], writes=[bankb[HB]])
                    S.add("act", lambda e, cc=cc: e.activation(
                        out=carry[par][:, cc, :], in_=banks[HB][:, 2 * cc:2 * cc + 2], func=AF.Identity, scale=hmask[:, 0:1]),
                        reads=[bankb[HB]], writes=[carb[par][cc]])
                yi = rot("yb", 4)
                yis.append(yi)
                S.add("act", lambda e, bk=bk, yi=yi, cc=cc: e.activation(
                    out=yb[yi][:, :], in_=banks[bk][:, :], func=AF.Identity, scale=V(V_FW + 2 * 88 + cc), bias=V(V_FB + cc)),
                    reads=[bankb[bk]], writes=[ybf[yi]])
                S.add("act", lambda e, bk=bk, cc=cc: e.copy(out=carry[1 - par][:, cc, :], in_=banks[bk][:, 510:512]),
                      reads=[bankb[bk]], writes=[carb[1 - par][cc]])
                S.add("dve", lambda e, bk=bk, yi=yi, cc=cc: e.scalar_tensor_tensor(
                    out=yb[yi][:, 1:512], in0=banks[bk][:, 0:511], scalar=V(V_FW + 88 + cc), in1=yb[yi][:, 1:512],
                    op0=ALU.mult, op1=ALU.add), reads=[bankb[bk], ybf[yi]], writes=[ybf[yi]])
                S.add("dve", lambda e, bk=bk, yi=yi, cc=cc: e.scalar_tensor_tensor(
                    out=yb[yi][:, 2:512], in0=banks[bk][:, 0:510], scalar=V(V_FW + cc), in1=yb[yi][:, 2:512],
                    op0=ALU.mult, op1=ALU.add), reads=[bankb[bk], ybf[yi]], writes=[ybf[yi]])
                S.add("dve", lambda e, yi=yi, cc=cc: e.scalar_tensor_tensor(
                    out=yb[yi][:, 0:2], in0=carry[par][:, cc, 0:2], scalar=V(V_FW + cc), in1=yb[yi][:, 0:2],
                    op0=ALU.mult, op1=ALU.add), reads=[carb[par][cc], ybf[yi]], writes=[ybf[yi]])
                S.add("dve", lambda e, yi=yi, cc=cc: e.scalar_tensor_tensor(
                    out=yb[yi][:, 0:1], in0=carry[par][:, cc, 1:2], scalar=V(V_FW + 88 + cc), in1=yb[yi][:, 0:1],
                    op0=ALU.mult, op1=ALU.add), reads=[carb[par][cc], ybf[yi]], writes=[ybf[yi]])
            si = rot("sb", 2)
            yg, yv = yis
            S.add("act", lambda e, yg=yg, si=si: e.activation(out=sbf[si][:, :], in_=yb[yg][:, :], func=AF.Silu),
                  reads=[ybf[yg]], writes=[sbb[si]])
            S.add("pool", lambda e, si=si, yv=yv, j=j: e.tensor_tensor(
                out=act[:, j, :], in0=sbf[si][:, :], in1=yb[yv][:, :], op=ALU.mult),
                reads=[sbb[si], ybf[yv]], writes=[actb[j]])
        if os.environ.get("MK_CUT", "") == "up":
            return
        host = (u + 1 < nunits) and u > 0 and os.environ.get("MK_NOHOST", "") == ""
        if host:
            x_load(u + 1, 0)
            x_load(u + 1, 1)
        for m in range(16):
            bk = next_bank("all")
            for t_ in range(3):
                nk = 16 if t_ < 2 else 12
                slot, slb = get_slab(sidx)
                sidx += 1
                for kk in range(nk):
                    k = 16 * t_ + kk
                    S.add("pe", lambda e, bk=bk, kk=kk, k=k, slot=slot: e.matmul(
                        banks[bk][:, :], lhsT=slot[:, kk * 128:(kk + 1) * 128], rhs=act[:, k, :],
                        start=(k == 0), stop=(k == NPAIR - 1)), reads=[slb, actb[k]], writes=[bankb[bk]])
            mi = m % 4
            S.add("dve", lambda e, bk=bk, m=m, mi=mi: e.tensor_tensor(
                out=oT[mi][:, :], in0=banks[bk][:, :], in1=xT[:, m, :], op=ALU.add),
                reads=[bankb[bk], xTb[m]], writes=[oTb[mi]])
            if mi == 3:
                for b in range(4):
                    bt = next_bank("all")
                    for mm in range(4):
                        S.add("pe", lambda e, bt=bt, mm=mm, b=b: e.transpose(
                            out=banks[bt][:, mm * 128:(mm + 1) * 128], in_=oT[mm][:, b * 128:(b + 1) * 128], identity=ident[:]),
                            reads=[oTb[mm]], writes=[bankb[bt]])
                    oi = rot("ost", 3)
                    S.add("act", lambda e, bt=bt, oi=oi: e.copy(out=ostage[oi][:, :], in_=banks[bt][:, :]),
                          reads=[bankb[bt]], writes=[ostb[oi]])
                    r_ = u * 512 + b * 128
                    c_ = (m - 3) * 128
                    if os.environ.get("MK_CUT", "") == "nostore":
                        continue
                    S.add("pool", lambda e, oi=oi, r_=r_, c_=c_: e.dma_start(
                        out=y_out[r_:r_ + 128, c_:c_ + 512], in_=ostage[oi][:, :]), reads=[ostb[oi]], key=f"ost{oi}")
                if host:
                    g_ = m // 4
                    head_group(u + 1, g_, "all")
                    if g_ + 2 < 4:
                        x_load(u + 1, g_ + 2)
        if host:
            head_finish(u + 1, "all")

    if nunits >= 0:
        mixer(-1)
    ALLE = ("pe", "act", "dve", "pool", "sp")
    for u in range(nunits):
        S.barrier(engines=ALLE if u == 1 else ALLE[:4])
        mixer(u)
        S.barrier(engines=ALLE if u == 0 else ALLE[:4])
        if os.environ.get("MK_CUT", "") == "mixer0":
            break
        ffn(u)

    keys = set(op.key for op in S.ops if op.key is not None)
    sems = {}
    for e in ("pe", "act", "dve", "pool"):
        sems[("e", e)] = nc.alloc_semaphore(f"s_{e}")
    for k in sorted(keys):
        sems[("k", k)] = nc.alloc_semaphore(f"k_{k}")
    nwait = S.emit(sems)
    return nc, len(S.ops), nwait


def _host_consts():
    c = np.zeros((128, 640), np.float32)
    c[:, 0:128] = np.eye(128, dtype=np.float32)
    c[:, 128:256] = 1.0
    c[0:64, 256:320] = 1.0
    c[64:128, 320:384] = 1.0
    kk = np.arange(128)[:, None]
    qq = np.arange(128)[None, :]
    if MASKMODE == "pe":
        NEG = -30000.0
        c[:, 384:512] = np.where(kk <= qq, 0.0, NEG)
        c[:, 512:640] = np.where(kk > qq, 0.0, NEG)
    else:
        c[:, 384:512] = (kk <= qq).astype(np.float32)
        c[:, 512:640] = (kk > qq).astype(np.float32)
    return c


def _host_vecs(inp):
    v = np.zeros((128, NV), np.float32)
    v[:, V_ANW:V_ANW + 16] = inp["attn_norm_w"][0].reshape(16, 128).T
    v[:, V_FNW:V_FNW + 16] = inp["ffn_norm_w"][0].reshape(16, 128).T
    v[:, V_QNW] = np.tile(inp["q_norm_w"][0], 2)
    v[:, V_KNW] = np.tile(inp["k_norm_w"][0], 2)
    cw = inp["conv_dw_w"][0]
    v[:, V_CW:V_CW + 248] = cw.reshape(CK, 8, 128).transpose(2, 1, 0).reshape(128, 248)
    v[:, V_CB:V_CB + 8] = inp["conv_dw_b"][0].reshape(8, 128).T
    v[:, V_LNW:V_LNW + 8] = inp["conv_ln_w"][0].reshape(8, 128).T
    v[:, V_LNB:V_LNB + 8] = inp["conv_ln_b"][0].reshape(8, 128).T
    fw = inp["ffn_dw_w"][0]
    v[:, V_FW:V_FW + 264] = fw.reshape(3, 88, 128).transpose(2, 0, 1).reshape(128, 264)
    v[:, V_FB:V_FB + 88] = inp["ffn_dw_b"][0].reshape(88, 128).T
    sk = inp["sinks"][0]
    for pp in range(2):
        for j in range(4):
            v[0:64, V_SINK + pp * 4 + j] = sk[8 * pp + j]
            v[64:128, V_SINK + pp * 4 + j] = sk[8 * pp + 4 + j]
    return v


_CACHE = {}


def kernel(**inputs):
    inp = {k: np.asarray(v) for k, v in inputs.items()}
    nunits = int(os.environ.get("MK_NUNITS", "8"))
    x = inp["x"].astype(np.float32, copy=False)
    if nunits not in _CACHE:
        _CACHE[nunits] = build_program(nunits)[0]
    nc = _CACHE[nunits]
    cst = _host_consts()
    vecs = _host_vecs(inp)
    w_in = np.ascontiguousarray(inp["w_in"][0])
    w_out = np.ascontiguousarray(inp["w_out"][0])
    w_up = np.ascontiguousarray(inp["w_up"][0])
    w_down = np.ascontiguousarray(inp["w_down"][0])
    in_maps = []
    for c in range(NCORES):
        b, q = divmod(c, 4)
        s0 = q * TOK
        xe = np.zeros((HALO + TOK, D), np.float32)
        if q > 0:
            xe[0:HALO] = x[b, s0 - HALO:s0]
        xe[HALO:] = x[b, s0:s0 + TOK]
        cc = np.zeros((128, 129), np.float32)
        cc[:, 0:128] = -30000.0 if MASKMODE == "pe" else 0.0
        if q > 0:
            cc[:, 0:128] = cst[:, 512:640]
            cc[:, 128] = 1.0
        in_maps.append({"x_ext": xe, "w_in": w_in, "w_out": w_out, "w_up": w_up, "w_down": w_down,
                        "vecs": vecs, "cst": cst, "ccore": cc})
    res = run_bass_kernel_spmd(nc, in_maps, core_ids=list(range(NCORES)))
    out = np.zeros((2, SEQ, D), np.float32)
    for c in range(NCORES):
        b, q = divmod(c, 4)
        out[b, q * TOK:(q + 1) * TOK] = res.results[c]["y_out"]
    return out
```

```python
import os
import numpy as np
import concourse.bass as bass
import concourse.mybir as mybir
from concourse.bass_utils import run_bass_kernel_spmd

F32 = mybir.dt.float32
BF16 = mybir.dt.bfloat16
AF = mybir.ActivationFunctionType
ALU = mybir.AluOpType

D = 2048
SEQ = 16384
NCORES = 8
TOK = 4096
HALO = 256
NH, NKV, DH = 16, 4, 64
QD, KVD = 1024, 256
CCH = 1024
CK = 31
INC = 3584
DFF = 5632
NPAIR = 44
EPS = 1e-6

IN0, OUT0, UP0, DN0 = 0, 28, 44, 132
NUNIT_SCR = 182
VW0 = 180

V_ANW, V_FNW, V_QNW, V_KNW = 0, 16, 32, 33
V_CW, V_CB, V_LNW, V_LNB = 34, 282, 290, 298
V_FW, V_FB, V_SINK = 306, 570, 658
NV = 666


class Buf:
    __slots__ = ("name", "w", "r")

    def __init__(self, name):
        self.name = name
        self.w = None
        self.r = []


class Op:
    __slots__ = ("eng", "fn", "deps", "inc", "cnt", "key")


class Sched:
    ENG = ("pe", "act", "dve", "pool", "sp")

    def __init__(self, nc):
        self.nc = nc
        self.ops = []
        self.pending = {e: set() for e in self.ENG}
        self.last = {e: None for e in self.ENG}
        self.lastkey = {}

    def add(self, eng, fn, reads=(), writes=(), key=None):
        op = Op()
        op.eng, op.fn, op.key, op.inc, op.cnt = eng, fn, key, False, 0
        idx = len(self.ops)
        deps = set()
        for b in reads:
            if b.w is not None:
                deps.add(b.w)
        for b in writes:
            if b.w is not None and not b.r:
                deps.add(b.w)
            lastr = {}
            for ri in b.r:
                p = self.ops[ri]
                if p.key is not None:
                    deps.add(ri)
                elif lastr.get(p.eng, -1) < ri:
                    lastr[p.eng] = ri
            deps.update(lastr.values())
        if self.pending[eng]:
            deps |= self.pending[eng]
            self.pending[eng] = set()
        real = set()
        for d in deps:
            p = self.ops[d]
            if eng == "pe" and p.eng == "pe" and p.key is None:
                continue
            real.add(d)
            if p.key is None:
                p.inc = True
        op.deps = real
        self.ops.append(op)
        for b in reads:
            b.r.append(idx)
        for b in writes:
            b.w = idx
            b.r = []
        if key is None:
            self.last[eng] = idx
        else:
            self.lastkey[key] = idx
        return idx

    def barrier(self, engines=("pe", "act", "dve", "pool")):
        deps = set()
        for e in ("pe", "act", "dve", "pool"):
            if self.last[e] is not None:
                deps.add(self.last[e])
                self.ops[self.last[e]].inc = True
        for k, i in self.lastkey.items():
            deps.add(i)
        for e in engines:
            self.pending[e] |= deps

    def emit(self, sems):
        nc = self.nc
        engs = {"pe": nc.tensor, "act": nc.scalar, "dve": nc.vector, "pool": nc.gpsimd, "sp": nc.sync}
        cnt = {e: 0 for e in self.ENG}
        kcnt = {}
        for op in self.ops:
            if op.key is not None:
                kcnt[op.key] = kcnt.get(op.key, 0) + 16
                op.cnt = kcnt[op.key]
            elif op.inc:
                cnt[op.eng] += 1
                op.cnt = cnt[op.eng]
        waited = {e: {} for e in self.ENG}
        nwait = 0
        for op in self.ops:
            eng = engs[op.eng]
            need = {}
            for d in op.deps:
                p = self.ops[d]
                s = ("k", p.key) if p.key is not None else ("e", p.eng)
                if need.get(s, 0) < p.cnt:
                    need[s] = p.cnt
            w = waited[op.eng]
            for s, v in need.items():
                if w.get(s, 0) < v:
                    eng.wait_ge(sems[s], v)
                    w[s] = v
                    nwait += 1
            if op.fn is None:
                continue
            ins = op.fn(eng)
            if op.key is not None:
                ins.then_inc(sems[("k", op.key)], 16)
            elif op.inc:
                ins.then_inc(sems[("e", op.eng)], 1)
        for k, v in kcnt.items():
            nc.sync.wait_ge(sems[("k", k)], v)
        return nwait


MASKMODE = os.environ.get("MK_MASK", "pe")
XQ = os.environ.get("MK_XQ", "act")


def build_program(nunits=8):
    nc = bass.Bass("TRN2", target_bir_lowering=False, dynamic_dma_scratch_size=2048)
    S = Sched(nc)

    x_ext = nc.dram_tensor("x_ext", [HALO + TOK, D], F32, kind="ExternalInput").ap()
    w_in = nc.dram_tensor("w_in", [D, INC], F32, kind="ExternalInput").ap()
    w_out = nc.dram_tensor("w_out", [D, D], F32, kind="ExternalInput").ap()
    w_up = nc.dram_tensor("w_up", [D, 2 * DFF], F32, kind="ExternalInput").ap()
    w_down = nc.dram_tensor("w_down", [DFF, D], F32, kind="ExternalInput").ap()
    vecs = nc.dram_tensor("vecs", [128, NV], F32, kind="ExternalInput").ap()
    cst = nc.dram_tensor("cst", [128, 640], F32, kind="ExternalInput").ap()
    ccore = nc.dram_tensor("ccore", [128, 129], F32, kind="ExternalInput").ap()
    y_out = nc.dram_tensor("y_out", [TOK, D], F32, kind="ExternalOutput").ap()
    wscr = nc.dram_tensor("wscr", [NUNIT_SCR, 128, 2048], BF16, kind="Internal").ap()

    off = [2048]

    def sb(name, shape, dt, at=None):
        nbytes = int(np.prod(shape[1:])) * (4 if dt == F32 else 2)
        if at is None:
            o = off[0]
            off[0] += (nbytes + 63) // 64 * 64
        else:
            o = at
        t = nc.alloc_sbuf_tensor_at(name, list(shape), dt, offset=o)
        return t, o, nbytes

    ident, _, _ = sb("ident", [128, 128], F32)
    ones_bf, _, _ = sb("ones_bf", [128, 128], BF16)
    blk_bf, _, _ = sb("blk_bf", [128, 128], BF16)
    maskC, _, _ = sb("maskC", [128, 4, 128], BF16)
    maskP, _, _ = sb("maskP", [128, 4, 128], BF16)
    maskP0, _, _ = sb("maskP0", [128, 4, 128], BF16)
    es = [sb(f"es{i}", [128, 4, 128], F32)[0] for i in range(2)]
    vec, _, _ = sb("vec", [128, NV], F32)
    hmask, _, _ = sb("hmask", [128, 1], F32)
    qsc, _, _ = sb("qsc", [128, 1], F32)
    essm, _, _ = sb("essm", [128, 8], F32)
    zer, _, _ = sb("zer", [128, 128], F32)
    xT, _, _ = sb("xT", [128, 16, 512], F32)
    hT, _, _ = sb("hT", [128, 16, 512], BF16)
    knT, _, _ = sb("knT", [128, 2, 640], BF16)
    Vr, _, _ = sb("Vr", [128, 5, 256], BF16)
    h2halo, _, _ = sb("h2halo", [128, 16, 2], BF16)
    carry = [sb(f"carry{i}", [128, 88, 2], F32)[0] for i in range(2)]
    ccarry, _, _ = sb("ccarry", [128, 8, 30], F32)
    NRING = 6
    wring = [sb(f"wring{i}", [128, 2048], BF16)[0] for i in range(NRING)]
    xs = [sb(f"xs{i}", [128, 2048], F32)[0] for i in range(2)]
    ident_bf, _, _ = sb("ident_bf", [128, 128], BF16)
    A0 = off[0]
    sq, _, _ = sb("sq", [128, 16, 512], BF16)
    stt = [sb(f"stt{i}", [128, 512], F32)[0] for i in range(4)]
    qnT, _, _ = sb("qnT", [128, 8, 512], BF16)
    sg = [sb(f"sg{i}", [128, 512], F32)[0] for i in range(2)]
    cbuf, _, _ = sb("cbuf", [128, 8, 544], F32)
    ybuf, _, _ = sb("ybuf", [128, 8, 512], F32)
    NE = 8
    Et = [sb(f"E{i}", [128, 4, 128], BF16)[0] for i in range(NE)]
    dtm = [sb(f"dtm{i}", [128, 512], F32)[0] for i in range(3)]
    A_END_M = off[0]
    off[0] = A0
    yb = [sb(f"yb{i}", [128, 512], F32)[0] for i in range(4)]
    sbf = [sb(f"sbf{i}", [128, 512], F32)[0] for i in range(2)]
    act, _, _ = sb("act", [128, NPAIR, 512], BF16)
    oT = [sb(f"oT{i}", [128, 512], F32)[0] for i in range(4)]
    ostage = [sb(f"ostage{i}", [128, 512], F32)[0] for i in range(3)]
    cvs = [sb(f"cvs{i}", [128, 1024], F32)[0] for i in range(3)]
    A_END_F = off[0]
    off[0] = A0
    pstage = [sb(f"pstage{i}", [128, 2048], F32)[0] for i in range(3)]
    pout = [sb(f"pout{i}", [128, 2048], BF16)[0] for i in range(3)]
    cstage, _, _ = sb("cstage", [128, 640], F32)
    ccstage, _, _ = sb("ccstage", [128, 129], F32)
    A_END_P = off[0]
    total = max(A_END_M, A_END_F, A_END_P)
    assert total <= 192 * 1024, total
    if os.environ.get("MK_VERBOSE"):
        print("SBUF: A0", A0, "mixer", A_END_M - A0, "ffn", A_END_F - A0, "pre", A_END_P - A0, "total", total)

    banks = [nc.alloc_psum_tensor(f"pb{i}", [128, 512], F32) for i in range(8)]
    bankb = [Buf(f"pb{i}") for i in range(8)]
    rr = {"main": 0, "all": 0}

    def next_bank(pool="main"):
        nb_ = 6 if pool == "main" else 7
        i = rr[pool] % nb_
        rr[pool] += 1
        return i

    xTb = [Buf(f"xT{k}") for k in range(16)]
    hTb = [Buf(f"hT{k}") for k in range(16)]
    sqb = [Buf(f"sq{k}") for k in range(16)]
    xsb = [Buf(f"xs{i}") for i in range(2)]
    kn_c = Buf("kn_c")
    kn_m = [Buf(f"kn_m{i}") for i in range(2)]
    vslot = [Buf(f"vs{i}") for i in range(5)]
    qnb = [Buf(f"qn{i}") for i in range(8)]
    cb_c = Buf("cb_c")
    ccb = Buf("ccarry")
    cb_m = [Buf(f"cb_m{i}") for i in range(8)]
    ybb = [Buf(f"ybuf{i}") for i in range(8)]
    sttb = [Buf(f"stt{i}") for i in range(4)]
    sgb = [Buf(f"sg{i}") for i in range(2)]
    Eb = [Buf(f"E{i}") for i in range(NE)]
    dtb = [Buf(f"dt{i}") for i in range(3)]
    h2hb = Buf("h2halo")
    carb = [[Buf(f"car{p}_{c}") for c in range(88)] for p in range(2)]
    ringb = [Buf(f"ring{i}") for i in range(NRING)]
    wvb = Buf("wv")
    ybf = [Buf(f"yb{i}") for i in range(4)]
    sbb = [Buf(f"sb{i}") for i in range(2)]
    actb = [Buf(f"act{j}") for j in range(NPAIR)]
    oTb = [Buf(f"oT{i}") for i in range(4)]
    ostb = [Buf(f"ost{i}") for i in range(3)]
    pstp = [[Buf(f"pst{i}_{l}") for l in range(6)] for i in range(3)]
    poutb = [Buf(f"pout{i}") for i in range(3)]
    cvsb = [Buf(f"cvs{i}") for i in range(3)]
    ctr = {"cvs": 0, "ring": 0, "stt": 0, "sg": 0, "E": 0, "dt": 0, "yb": 0, "sb": 0, "ost": 0, "pp": 0, "xs": 0}

    def rot(name, n):
        i = ctr[name] % n
        ctr[name] += 1
        return i

    cb_ = Buf("cstage")
    S.add("sp", lambda e: e.dma_start(out=cstage[:], in_=cst[:, :]), writes=[cb_], key="const0")
    ccb_ = Buf("ccstage")
    S.add("sp", lambda e: e.dma_start(out=ccstage[:], in_=ccore[:, :]), writes=[ccb_], key="const1")
    vecb = Buf("vec")
    S.add("sp", lambda e: e.dma_start(out=vec[:], in_=vecs[:, :]), writes=[vecb], key="const2")
    cbuf_ = Buf("consts")
    S.add("dve", lambda e: e.tensor_copy(out=ident[:], in_=cstage[:, 0:128]), reads=[cb_], writes=[cbuf_])
    S.add("dve", lambda e: e.tensor_copy(out=ones_bf[:], in_=cstage[:, 128:256]), reads=[cb_], writes=[cbuf_])
    S.add("dve", lambda e: e.tensor_copy(out=ident_bf[:], in_=cstage[:, 0:128]), reads=[cb_], writes=[cbuf_])
    S.add("dve", lambda e: e.tensor_copy(out=blk_bf[:], in_=cstage[:, 256:384]), reads=[cb_], writes=[cbuf_])
    for j in range(4):
        S.add("dve", lambda e, j=j: e.tensor_copy(out=maskC[:, j, :], in_=cstage[:, 384:512]), reads=[cb_], writes=[cbuf_])
        S.add("dve", lambda e, j=j: e.tensor_copy(out=maskP[:, j, :], in_=cstage[:, 512:640]), reads=[cb_], writes=[cbuf_])
        S.add("dve", lambda e, j=j: e.tensor_copy(out=maskP0[:, j, :], in_=ccstage[:, 0:128]), reads=[ccb_], writes=[cbuf_])
    S.add("dve", lambda e: e.tensor_copy(out=hmask[:], in_=ccstage[:, 128:129]), reads=[ccb_], writes=[cbuf_])
    S.add("dve", lambda e: e.memset(zer[:], 0.0), writes=[cbuf_])
    S.add("act", lambda e: e.activation(out=qsc[:], in_=vec[:, V_QNW:V_QNW + 1], func=AF.Identity, scale=0.125), reads=[vecb], writes=[cbuf_])
    S.add("act", lambda e: e.activation(out=essm[:], in_=vec[:, V_SINK:V_SINK + 8], func=AF.Exp), reads=[vecb], writes=[cbuf_])
    for pp in range(2):
        for j in range(4):
            S.add("dve", lambda e, pp=pp, j=j: e.tensor_scalar(
                out=es[pp][:, j, :], in0=zer[:], scalar1=essm[:, pp * 4 + j:pp * 4 + j + 1], scalar2=None, op0=ALU.add),
                reads=[cbuf_], writes=[cbuf_])
    S.add("pool", lambda e: e.memset(knT[:, :, 0:128], 0.0), writes=[kn_c])
    S.add("pool", lambda e: e.memset(Vr[:, 0, :], 0.0), writes=[vslot[0]])

    win_v = w_in.rearrange("(k p) c -> p k c", p=128)
    wup_v = w_up.rearrange("(k p) c -> p k c", p=128)
    wdn_v = w_down.rearrange("(k p) c -> p k c", p=128)
    woc_v = w_out[1024:2048, :].rearrange("(k p) c -> p k c", p=128)
    woa_v = w_out[0:1024, :].rearrange("(pp e j r) c -> e r pp j c", pp=2, e=2, j=4, r=64)
    cast_eng = ["act", "dve", "act", "dve", "pool"]

    def prepass_unit(loads, nk, dst):
        i = rot("pp", 3)
        for li, ld in enumerate(loads):
            S.add("sp", lambda e, ld=ld, i=i: ld(e, pstage[i]), writes=[pstp[i][li]], key=f"pin{i}")
        rd = pstp[i][0:len(loads)]
        ce = cast_eng[ctr["pp"] % len(cast_eng)]
        if ce == "act":
            S.add("act", lambda e, i=i: e.copy(out=pout[i][:, 0:nk * 128], in_=pstage[i][:, 0:nk * 128]),
                  reads=rd, writes=[poutb[i]])
        else:
            S.add(ce, lambda e, i=i: e.tensor_copy(out=pout[i][:, 0:nk * 128], in_=pstage[i][:, 0:nk * 128]),
                  reads=rd, writes=[poutb[i]])
        S.add("act", lambda e, i=i: e.dma_start(out=dst, in_=pout[i][:, 0:nk * 128]), reads=[poutb[i]], key=f"pout{i}")

    def st3(t, nk=16):
        return t[:, 0:nk * 128].rearrange("p (k c) -> p k c", c=128)

    in_order = []
    for i in range(8):
        in_order += [("a", i), ("g", i)]
    in_order += [("k", 0), ("k", 1)] + [("q", c) for c in range(8)]
    for s, (kind, i) in enumerate(in_order):
        if kind == "q":
            pp, j = divmod(i, 4)
            lds = []
            for e_ in range(2):
                head = 8 * pp + 4 * e_ + j
                lds.append(lambda e, t, head=head, e_=e_: e.dma_start(
                    out=st3(t)[:, :, e_ * 64:(e_ + 1) * 64], in_=win_v[:, :, head * 64:(head + 1) * 64]))
        else:
            c0_ = {"a": 1536, "g": 2560, "k": 1024}[kind] + i * 128
            lds = [lambda e, t, c0_=c0_: e.dma_start(out=st3(t), in_=win_v[:, :, c0_:c0_ + 128])]
        prepass_unit(lds, 16, wscr[IN0 + s])
    for h in range(2):
        lds = [lambda e, t, h=h: e.dma_start(out=t[:, :].rearrange("p (k c) -> p k c", c=256),
                                              in_=win_v[:, h * 8:(h + 1) * 8, 1280:1536])]
        prepass_unit(lds, 16, wscr[VW0 + h])
    for m in range(16):
        lds = []
        for e_ in range(2):
            for pp_ in range(2):
                lds.append(lambda e, t, m=m, e_=e_, pp_=pp_: e.dma_start(
                    out=st3(t)[e_ * 64:(e_ + 1) * 64, 4 * pp_:4 * pp_ + 4, :],
                    in_=woa_v[e_, :, pp_, :, m * 128:(m + 1) * 128]))
        lds.append(lambda e, t, m=m: e.dma_start(out=st3(t)[:, 8:16, :], in_=woc_v[:, :, m * 128:(m + 1) * 128]))
        prepass_unit(lds, 16, wscr[OUT0 + m])
    S.barrier(engines=("pe", "act", "dve", "pool", "sp"))

    def load_unit(scr_idx, ncols=2048):
        i = rot("ring", NRING)
        S.add("sp", lambda e, i=i: e.dma_start(out=wring[i][:, 0:ncols], in_=wscr[scr_idx][:, 0:ncols]),
              writes=[ringb[i]], key=f"ring{i}")
        return wring[i], ringb[i]

    def load_unit_cvt(scr_idx, nk, src):
        i = rot("ring", NRING)
        for h in range(2):
            k0, k1 = 8 * h, min(nk, 8 * h + 8)
            if k1 <= k0:
                continue
            si = rot("cvs", 3)
            if src[0] == "up":
                ap = wup_v[:, k0:k1, src[1] * 128:(src[1] + 1) * 128]
            else:
                ap = wdn_v[:, 16 * src[2] + k0:16 * src[2] + k1, src[1] * 128:(src[1] + 1) * 128]
            S.add("sp", lambda e, si=si, ap=ap, k0=k0, k1=k1: e.dma_start(
                out=cvs[si][:, 0:(k1 - k0) * 128].rearrange("p (k c) -> p k c", c=128), in_=ap),
                writes=[cvsb[si]], key=f"cvs{si}")
            S.add("act", lambda e, si=si, i=i, k0=k0, k1=k1: e.copy(
                out=wring[i][:, k0 * 128:k1 * 128], in_=cvs[si][:, 0:(k1 - k0) * 128]),
                reads=[cvsb[si]], writes=[ringb[i]])
        S.add("act", lambda e, i=i: e.dma_start(out=wscr[scr_idx][:, 0:nk * 128], in_=wring[i][:, 0:nk * 128]),
              reads=[ringb[i]], key=f"rst{i}")
        return wring[i], ringb[i]

    def V(c, n=1):
        return vec[:, c:c + n]

    conv_state = {"glu": 0, "gen": None}

    xslots = {}
    head_done = {}

    def unit_geo(u):
        n = 256 if u < 0 else 512
        r0 = 0 if u < 0 else HALO + 512 * u
        return n, n // 128, r0

    def x_load(u, g):
        n, nb, r0 = unit_geo(u)
        xi = rot("xs", 2)
        S.add(XQ, lambda e: e.dma_start(
            out=xs[xi][:, 0:nb * 512].rearrange("p (b c) -> p b c", c=512),
            in_=x_ext[r0:r0 + n, 512 * g:512 * g + 512].rearrange("(b p) c -> p b c", p=128)),
            writes=[xsb[xi]], key=f"xs{xi}")
        xslots[(u, g)] = xi

    def head_group(u, g, pool):
        n, nb, r0 = unit_geo(u)
        xi = xslots.pop((u, g))
        for kk in range(4):
            k = 4 * g + kk
            bk = next_bank(pool)
            for b in range(nb):
                S.add("pe", lambda e, bk=bk, kk=kk, b=b: e.transpose(
                    out=banks[bk][:, b * 128:(b + 1) * 128], in_=xs[xi][:, b * 512 + kk * 128:b * 512 + (kk + 1) * 128],
                    identity=ident[:]), reads=[xsb[xi]], writes=[bankb[bk]])
            S.add("act", lambda e, bk=bk, k=k: e.copy(out=xT[:, k, 0:n], in_=banks[bk][:, 0:n]),
                  reads=[bankb[bk]], writes=[xTb[k]])
            S.add("act", lambda e, bk=bk, k=k: e.activation(out=sq[:, k, 0:n], in_=banks[bk][:, 0:n], func=AF.Square),
                  reads=[bankb[bk]], writes=[sqb[k]])

    def head_finish(u, pool):
        n, nb, r0 = unit_geo(u)
        bs = next_bank(pool)
        for k in range(16):
            S.add("pe", lambda e, k=k: e.matmul(banks[bs][:, 0:n], lhsT=ones_bf[:], rhs=sq[:, k, 0:n],
                                                start=(k == 0), stop=(k == 15)),
                  reads=[sqb[k]], writes=[bankb[bs]])
        t1 = rot("stt", 4)
        S.add("act", lambda e: e.activation(out=stt[t1][:, 0:n], in_=banks[bs][:, 0:n], func=AF.Ln,
                                            scale=1.0 / D, bias=EPS), reads=[bankb[bs]], writes=[sttb[t1]])
        S.add("act", lambda e: e.activation(out=banks[bs][:, 0:n], in_=stt[t1][:, 0:n], func=AF.Exp, scale=-0.5),
              reads=[sttb[t1]], writes=[bankb[bs]])
        for k in range(16):
            S.add("dve", lambda e, k=k: e.scalar_tensor_tensor(
                out=hT[:, k, 0:n], in0=xT[:, k, 0:n], scalar=V(V_ANW + k), in1=banks[bs][:, 0:n],
                op0=ALU.mult, op1=ALU.mult), reads=[xTb[k], bankb[bs]], writes=[hTb[k]])
        head_done[u] = True

    def head_inline(u):
        x_load(u, 0)
        x_load(u, 1)
        for g in range(4):
            head_group(u, g, "main")
            if g + 2 < 4:
                x_load(u, g + 2)
        head_finish(u, "main")

    def mixer(u):
        n = 256 if u < 0 else 512
        c0 = 128 if u < 0 else 0
        w = n - c0
        nb = n // 128
        r0 = 0 if u < 0 else HALO + 512 * u
        if u >= 0:
            pn = 256 if u == 0 else 512
            pnb = pn // 128
            S.add("pool", lambda e: e.tensor_copy(out=knT[:, :, 0:128], in_=knT[:, :, pn:pn + 128]),
                  reads=kn_m, writes=[kn_c])
            S.add("pool", lambda e: e.tensor_copy(out=Vr[:, 0, :], in_=Vr[:, pnb, :]), reads=[vslot[pnb]], writes=[vslot[0]])
            S.add("pool", lambda e: e.tensor_copy(out=cbuf[:, :, 0:30], in_=ccarry[:, :, :]), reads=[ccb], writes=[cb_c])
        else:
            S.add("pool", lambda e: e.memset(cbuf[:, :, 0:30], 0.0), writes=[cb_c])
        if not head_done.get(u):
            head_inline(u)

        NPT = int(os.environ.get("MK_NPT", "9"))

        def conv_gen():
            for i in range(8):
                while conv_state["glu"] <= i:
                    yield False
                cvb = 6 + (i % 2)
                for k in range(NPT):
                    if k == 0:
                        S.add("pool", lambda e, i=i, k=k: e.tensor_scalar(
                            out=ybuf[:, i, c0:n], in0=cbuf[:, i, c0 + k:n + k], scalar1=V(V_CW + i * 31 + k), scalar2=0.0,
                            op0=ALU.mult, op1=ALU.add), reads=[cb_m[i], cb_c], writes=[ybb[i]])
                    else:
                        S.add("pool", lambda e, i=i, k=k: e.tensor_scalar(
                            out=dtm[2][:, 0:w], in0=cbuf[:, i, c0 + k:n + k], scalar1=V(V_CW + i * 31 + k), scalar2=0.0,
                            op0=ALU.mult, op1=ALU.add), reads=[cb_m[i], cb_c], writes=[dtb[2]])
                        S.add("pool", lambda e, i=i: e.tensor_tensor(
                            out=ybuf[:, i, c0:n], in0=ybuf[:, i, c0:n], in1=dtm[2][:, 0:w], op=ALU.add),
                            reads=[dtb[2], ybb[i]], writes=[ybb[i]])
                S.add("act", lambda e, i=i, cvb=cvb: e.activation(
                    out=banks[cvb][:, 0:w], in_=cbuf[:, i, c0 + 30:n + 30], func=AF.Identity,
                    scale=V(V_CW + i * 31 + 30), bias=V(V_CB + i)),
                    reads=[cb_m[i]], writes=[bankb[cvb]])
                yield True
                for k in range(29, NPT - 1, -1):
                    S.add("dve", lambda e, i=i, k=k, cvb=cvb: e.scalar_tensor_tensor(
                        out=banks[cvb][:, 0:w], in0=cbuf[:, i, c0 + k:n + k], scalar=V(V_CW + i * 31 + k),
                        in1=banks[cvb][:, 0:w], op0=ALU.mult, op1=ALU.add),
                        reads=[cb_m[i], cb_c], writes=[bankb[cvb]])
                    yield True
                if NPT > 0:
                    S.add("dve", lambda e, i=i, cvb=cvb: e.tensor_tensor(
                        out=ybuf[:, i, c0:n], in0=banks[cvb][:, 0:w], in1=ybuf[:, i, c0:n], op=ALU.add),
                        reads=[bankb[cvb], ybb[i]], writes=[ybb[i]])
                else:
                    S.add("act", lambda e, i=i, cvb=cvb: e.copy(out=ybuf[:, i, c0:n], in_=banks[cvb][:, 0:w]),
                          reads=[bankb[cvb]], writes=[ybb[i]])
                S.add("act", lambda e, i=i: e.copy(out=sq[:, i, c0:n], in_=ybuf[:, i, c0:n]),
                      reads=[ybb[i]], writes=[sqb[i]])
                S.add("act", lambda e, i=i: e.activation(out=sq[:, 8 + i, c0:n], in_=ybuf[:, i, c0:n], func=AF.Square),
                      reads=[ybb[i]], writes=[sqb[8 + i]])
                yield True

        conv_state["glu"] = 0
        cg = conv_gen()
        cdone = [False]

        def conv_adv(k):
            if cdone[0]:
                return
            for _ in range(k):
                try:
                    r = next(cg)
                except StopIteration:
                    cdone[0] = True
                    return
                if r is False:
                    return

        deferred = []
        pend_a = {}
        for s, (kind, i) in enumerate(in_order):
            slot, slb = load_unit(IN0 + s)
            bk = next_bank()
            for k in range(16):
                S.add("pe", lambda e, bk=bk, k=k, slot=slot: e.matmul(
                    banks[bk][:, 0:n], lhsT=slot[:, k * 128:(k + 1) * 128], rhs=hT[:, k, 0:n],
                    start=(k == 0), stop=(k == 15)), reads=[slb, hTb[k]], writes=[bankb[bk]])
            for f in deferred:
                f()
            deferred = []
            if kind == "a":
                pend_a[i] = bk
            elif kind == "g":
                ba = pend_a.pop(i)
                sgi = rot("sg", 2)
                S.add("act", lambda e, bk=bk, sgi=sgi: e.activation(out=sg[sgi][:, 0:n], in_=banks[bk][:, 0:n],
                                                                   func=AF.Sigmoid),
                      reads=[bankb[bk]], writes=[sgb[sgi]])
                S.add("dve", lambda e, ba=ba, sgi=sgi, i=i: e.tensor_tensor(
                    out=cbuf[:, i, 30:30 + n], in0=banks[ba][:, 0:n], in1=sg[sgi][:, 0:n], op=ALU.mult),
                    reads=[bankb[ba], sgb[sgi]], writes=[cb_m[i]])
                conv_state["glu"] = i + 1
            else:
                ei = rot("E", NE)
                sqq = Et[ei][:, :, :].rearrange("p a c -> p (a c)")
                S.add("act", lambda e, bk=bk, sqq=sqq: e.activation(out=sqq[:, 0:n], in_=banks[bk][:, 0:n], func=AF.Square),
                      reads=[bankb[bk]], writes=[Eb[ei]])

                def post(bk=bk, ei=ei, sqq=sqq, kind=kind, i=i):
                    b2 = next_bank()
                    S.add("pe", lambda e: e.matmul(banks[b2][:, 0:n], lhsT=blk_bf[:], rhs=sqq[:, 0:n], start=True, stop=True),
                          reads=[Eb[ei]], writes=[bankb[b2]])
                    t2 = rot("stt", 4)
                    S.add("act", lambda e: e.activation(out=stt[t2][:, 0:n], in_=banks[b2][:, 0:n], func=AF.Ln,
                                                        scale=1.0 / DH, bias=EPS), reads=[bankb[b2]], writes=[sttb[t2]])
                    S.add("act", lambda e: e.activation(out=stt[t2][:, 0:n], in_=stt[t2][:, 0:n], func=AF.Exp, scale=-0.5),
                          reads=[sttb[t2]], writes=[sttb[t2]])
                    if kind == "q":
                        S.add("dve", lambda e: e.scalar_tensor_tensor(
                            out=qnT[:, i, 0:n], in0=banks[bk][:, 0:n], scalar=qsc[:, 0:1], in1=stt[t2][:, 0:n],
                            op0=ALU.mult, op1=ALU.mult), reads=[bankb[bk], sttb[t2]], writes=[qnb[i]])
                    else:
                        S.add("dve", lambda e: e.scalar_tensor_tensor(
                            out=knT[:, i, 128:128 + n], in0=banks[bk][:, 0:n], scalar=V(V_KNW), in1=stt[t2][:, 0:n],
                            op0=ALU.mult, op1=ALU.mult), reads=[bankb[bk], sttb[t2]], writes=[kn_m[i]])
                deferred.append(post)
            conv_adv(6)
        vsl = [load_unit(VW0 + h_) for h_ in range(2)]
        for b in range(nb):
            bk = next_bank()
            for k in range(16):
                h_, kk_ = divmod(k, 8)
                S.add("pe", lambda e, bk=bk, k=k, b=b, h_=h_, kk_=kk_: e.matmul(
                    banks[bk][:, 0:256], lhsT=hT[:, k, b * 128:(b + 1) * 128],
                    rhs=vsl[h_][0][:, kk_ * 256:(kk_ + 1) * 256], start=(k == 0), stop=(k == 15)),
                    reads=[vsl[h_][1], hTb[k]], writes=[bankb[bk]])
            if b == 0:
                for f in deferred:
                    f()
                deferred = []
            S.add("act", lambda e, bk=bk, b=b: e.copy(out=Vr[:, b + 1, :], in_=banks[bk][:, 0:256]),
                  reads=[bankb[bk]], writes=[vslot[b + 1]])
            conv_adv(4)

        S.add("pool", lambda e: e.tensor_copy(out=ccarry[:, :, :], in_=cbuf[:, :, n:n + 30]), reads=cb_m, writes=[ccb])
        def att_phase1(qb, pp):
            items = []
            for e_ in range(2):
                g = 2 * pp + e_
                for kbi in range(2):
                    kcol = qb * 128 + kbi * 128
                    vs_ = qb + kbi
                    if kbi == 0:
                        mk = maskP0 if (u == 0 and qb == 0) else maskP
                        knr = [kn_c] if qb == 0 else [kn_m[pp]]
                    else:
                        mk = maskC
                        knr = [kn_m[pp]]
                    sbk = next_bank()
                    ei = rot("E", NE)
                    S.add("pe", lambda e, sbk=sbk, e_=e_, pp=pp, kcol=kcol, qb=qb: e.matmul(
                        banks[sbk][:, :].rearrange("p (a c) -> p a c", c=128),
                        lhsT=knT[e_ * 64:(e_ + 1) * 64, pp, kcol:kcol + 128],
                        rhs=qnT[e_ * 64:(e_ + 1) * 64, 4 * pp:4 * pp + 4, qb * 128:(qb + 1) * 128],
                        start=True, stop=(MASKMODE != "pe")),
                        reads=knr + [qnb[4 * pp + j] for j in range(4)], writes=[bankb[sbk]])
                    if MASKMODE == "pe":
                        S.add("pe", lambda e, sbk=sbk, mk=mk: e.matmul(
                            banks[sbk][:, :].rearrange("p (a c) -> p a c", c=128), lhsT=ident_bf[:], rhs=mk[:, :, :],
                            start=False, stop=True), reads=[], writes=[bankb[sbk]])
                    S.add("act", lambda e, sbk=sbk, ei=ei: e.activation(
                        out=Et[ei][:, :, :], in_=banks[sbk][:, :].rearrange("p (a c) -> p a c", c=128), func=AF.Exp),
                        reads=[bankb[sbk]], writes=[Eb[ei]])
                    if MASKMODE != "pe":
                        S.add("pool" if MASKMODE == "pool" else "dve", lambda e, ei=ei, mk=mk: e.tensor_tensor(
                            out=Et[ei][:, :, :], in0=Et[ei][:, :, :], in1=mk[:, :, :], op=ALU.mult),
                            reads=[Eb[ei]], writes=[Eb[ei]])
                    items.append((e_, g, kbi, vs_, ei))
            return items

        def att_phase2(qb, pp, items):
            pvb = next_bank()
            smb = next_bank()
            for (e_, g, kbi, vs_, ei) in items:
                S.add("pe", lambda e, pvb=pvb, e_=e_, g=g, vs_=vs_, ei=ei, kbi=kbi: e.matmul(
                    banks[pvb][e_ * 64:(e_ + 1) * 64, :], lhsT=Vr[:, vs_, g * 64:(g + 1) * 64],
                    rhs=Et[ei][:, :, :].rearrange("p a c -> p (a c)"), start=(kbi == 0), stop=(kbi == 1)),
                    reads=[vslot[vs_], Eb[ei]], writes=[bankb[pvb]])
                S.add("pe", lambda e, smb=smb, e_=e_, ei=ei, kbi=kbi: e.matmul(
                    banks[smb][e_ * 64:(e_ + 1) * 64, :], lhsT=ones_bf[:, 0:64],
                    rhs=Et[ei][:, :, :].rearrange("p a c -> p (a c)"), start=(kbi == 0), stop=(kbi == 1)),
                    reads=[Eb[ei]], writes=[bankb[smb]])
            di = rot("dt", 2)
            S.add("dve", lambda e, smb=smb, di=di, pp=pp: e.tensor_tensor(
                out=dtm[di][:, :], in0=banks[smb][:, :], in1=es[pp][:, :, :].rearrange("p a c -> p (a c)"), op=ALU.add),
                reads=[bankb[smb]], writes=[dtb[di]])
            S.add("act", lambda e, di=di: e.activation(out=dtm[di][:, :], in_=dtm[di][:, :], func=AF.Ln),
                  reads=[dtb[di]], writes=[dtb[di]])
            S.add("act", lambda e, di=di: e.activation(out=dtm[di][:, :], in_=dtm[di][:, :], func=AF.Exp, scale=-1.0),
                  reads=[dtb[di]], writes=[dtb[di]])
            S.add("dve", lambda e, pvb=pvb, di=di, pp=pp, qb=qb: e.tensor_tensor(
                out=hT[:, 4 * pp:4 * pp + 4, qb * 128:(qb + 1) * 128],
                in0=banks[pvb][:, :].rearrange("p (a c) -> p a c", c=128),
                in1=dtm[di][:, :].rearrange("p (a c) -> p a c", c=128), op=ALU.mult),
                reads=[bankb[pvb], dtb[di]], writes=[hTb[4 * pp + j] for j in range(4)])
            conv_adv(8)

        prev_step = None
        for qb in range(c0 // 128, nb):
            for pp in range(2):
                items = att_phase1(qb, pp)
                if prev_step is not None:
                    att_phase2(*prev_step)
                prev_step = (qb, pp, items)
        att_phase2(*prev_step)
        while not cdone[0]:
            conv_adv(1000)

        bM = next_bank()
        bQ = next_bank()
        for i in range(8):
            S.add("pe", lambda e, i=i: e.matmul(banks[bM][:, 0:w], lhsT=ones_bf[:], rhs=sq[:, i, c0:n],
                                                start=(i == 0), stop=(i == 7)), reads=[sqb[i]], writes=[bankb[bM]])
        for i in range(8):
            S.add("pe", lambda e, i=i: e.matmul(banks[bQ][:, 0:w], lhsT=ones_bf[:], rhs=sq[:, 8 + i, c0:n],
                                                start=(i == 0), stop=(i == 7)), reads=[sqb[8 + i]], writes=[bankb[bQ]])
        t3 = rot("stt", 4)
        S.add("act", lambda e: e.activation(out=stt[t3][:, 0:w], in_=banks[bM][:, 0:w], func=AF.Square, scale=1.0 / CCH),
              reads=[bankb[bM]], writes=[sttb[t3]])
        S.add("dve", lambda e: e.scalar_tensor_tensor(
            out=stt[t3][:, 0:w], in0=banks[bQ][:, 0:w], scalar=1.0 / CCH, in1=stt[t3][:, 0:w],
            op0=ALU.mult, op1=ALU.subtract), reads=[bankb[bQ], sttb[t3]], writes=[sttb[t3]])
        S.add("act", lambda e: e.activation(out=stt[t3][:, 0:w], in_=stt[t3][:, 0:w], func=AF.Ln, bias=EPS),
              reads=[sttb[t3]], writes=[sttb[t3]])
        S.add("act", lambda e: e.activation(out=banks[bQ][:, 0:w], in_=stt[t3][:, 0:w], func=AF.Exp, scale=-0.5),
              reads=[sttb[t3]], writes=[bankb[bQ]])
        for i in range(8):
            d1 = rot("dt", 2)
            S.add("dve", lambda e, i=i, d1=d1: e.scalar_tensor_tensor(
                out=dtm[d1][:, 0:w], in0=banks[bM][:, 0:w], scalar=-1.0 / CCH, in1=ybuf[:, i, c0:n],
                op0=ALU.mult, op1=ALU.add), reads=[bankb[bM], ybb[i]], writes=[dtb[d1]])
            S.add("dve", lambda e, d1=d1: e.tensor_tensor(
                out=dtm[d1][:, 0:w], in0=dtm[d1][:, 0:w], in1=banks[bQ][:, 0:w], op=ALU.mult),
                reads=[dtb[d1], bankb[bQ]], writes=[dtb[d1]])
            S.add("act", lambda e, i=i, d1=d1: e.activation(
                out=hT[:, 8 + i, c0:n], in_=dtm[d1][:, 0:w], func=AF.Silu, scale=V(V_LNW + i), bias=V(V_LNB + i)),
                reads=[dtb[d1]], writes=[hTb[8 + i]])

        for m in range(16):
            slot, slb = load_unit(OUT0 + m)
            bk = next_bank()
            for k in range(16):
                S.add("pe", lambda e, bk=bk, k=k, slot=slot: e.matmul(
                    banks[bk][:, 0:w], lhsT=slot[:, k * 128:(k + 1) * 128], rhs=hT[:, k, c0:n],
                    start=(k == 0), stop=(k == 15)), reads=[slb, hTb[k]], writes=[bankb[bk]])
            S.add("dve", lambda e, bk=bk, m=m: e.tensor_tensor(
                out=xT[:, m, c0:n], in0=banks[bk][:, 0:w], in1=xT[:, m, c0:n], op=ALU.add),
                reads=[bankb[bk], xTb[m]], writes=[xTb[m]])
            S.add("act", lambda e, m=m: e.activation(out=sq[:, m, c0:n], in_=xT[:, m, c0:n], func=AF.Square),
                  reads=[xTb[m]], writes=[sqb[m]])
        bs2 = next_bank()
        for m in range(16):
            S.add("pe", lambda e, m=m: e.matmul(banks[bs2][:, 0:w], lhsT=ones_bf[:], rhs=sq[:, m, c0:n],
                                                start=(m == 0), stop=(m == 15)), reads=[sqb[m]], writes=[bankb[bs2]])
        t4 = rot("stt", 4)
        S.add("act", lambda e: e.activation(out=stt[t4][:, 0:w], in_=banks[bs2][:, 0:w], func=AF.Ln,
                                            scale=1.0 / D, bias=EPS), reads=[bankb[bs2]], writes=[sttb[t4]])
        S.add("act", lambda e: e.activation(out=banks[bs2][:, 0:w], in_=stt[t4][:, 0:w], func=AF.Exp, scale=-0.5),
              reads=[sttb[t4]], writes=[bankb[bs2]])
        for m in range(16):
            S.add("dve", lambda e, m=m: e.scalar_tensor_tensor(
                out=hT[:, m, c0:n], in0=xT[:, m, c0:n], scalar=V(V_FNW + m), in1=banks[bs2][:, 0:w],
                op0=ALU.mult, op1=ALU.mult), reads=[xTb[m], bankb[bs2]], writes=[hTb[m]])
        if u < 0:
            S.add("pool", lambda e: e.tensor_copy(out=h2halo[:, :, :], in_=hT[:, :, n - 2:n]), reads=hTb, writes=[h2hb])
            pass

    HB = 7

    def ffn(u):
        par = u % 2
        order = []
        for j in range(NPAIR):
            for cc in (j, NPAIR + j):
                order.append((UP0 + cc, 16, ("up", cc)))
        for m in range(16):
            for t_ in range(3):
                order.append((DN0 + 3 * m + t_, 16 if t_ < 2 else 12, ("dn", m, t_)))
        slots = {}
        nxt = [0]
        look = 3 if u == 0 else 0

        def get_slab(si_):
            while nxt[0] <= min(si_ + look, len(order) - 1):
                scr, nk_, src = order[nxt[0]]
                slots[nxt[0]] = load_unit_cvt(scr, nk_, src) if u == 0 else load_unit(scr, nk_ * 128)
                nxt[0] += 1
            return slots.pop(si_)

        sidx = 0
        for j in range(NPAIR):
            yis = []
            for cc in (j, NPAIR + j):
                slot, slb = get_slab(sidx)
                sidx += 1
                bk = next_bank("all")
                for k in range(16):
                    S.add("pe", lambda e, bk=bk, k=k, slot=slot: e.matmul(
                        banks[bk][:, :], lhsT=slot[:, k * 128:(k + 1) * 128], rhs=hT[:, k, :],
                        start=(k == 0), stop=(k == 15)), reads=[slb, hTb[k]], writes=[bankb[bk]])
                if u == 0:
                    for k in range(16):
                        S.add("pe", lambda e, k=k, slot=slot, cc=cc: e.matmul(
                            banks[HB][:, 2 * cc:2 * cc + 2], lhsT=slot[:, k * 128:(k + 1) * 128], rhs=h2halo[:, k, :],
                            start=(k == 0), stop=(k == 15)), reads=[slb, h2hb], writes=[bankb[HB]])
                    S.add("act", lambda e, cc=cc: e.activation(
                        out=carry[par][:, cc, :], in_=banks[HB][:, 2 * cc:2 * cc + 2], func=AF.Identity, scale=hmask[:, 0:1]),
                        reads=[bankb[HB]], writes=[carb[par][cc]])
                yi = rot("yb", 4)
                yis.append(yi)
                S.add("act", lambda e, bk=bk, yi=yi, cc=cc: e.activation(
                    out=yb[yi][:, :], in_=banks[bk][:, :], func=AF.Identity, scale=V(V_FW + 2 * 88 + cc), bias=V(V_FB + cc)),
                    reads=[bankb[bk]], writes=[ybf[yi]])
                S.add("act", lambda e, bk=bk, cc=cc: e.copy(out=carry[1 - par][:, cc, :], in_=banks[bk][:, 510:512]),
                      reads=[bankb[bk]], writes=[carb[1 - par][cc]])
                S.add("dve", lambda e, bk=bk, yi=yi, cc=cc: e.scalar_tensor_tensor(
                    out=yb[yi][:, 1:512], in0=banks[bk][:, 0:511], scalar=V(V_FW + 88 + cc), in1=yb[yi][:, 1:512],
                    op0=ALU.mult, op1=ALU.add), reads=[bankb[bk], ybf[yi]], writes=[ybf[yi]])
                S.add("dve", lambda e, bk=bk, yi=yi, cc=cc: e.scalar_tensor_tensor(
                    out=yb[yi][:, 2:512], in0=banks[bk][:, 0:510], scalar=V(V_FW + cc), in1=yb[yi][:, 2:512],
                    op0=ALU.mult, op1=ALU.add), reads=[bankb[bk], ybf[yi]], writes=[ybf[yi]])
                S.add("dve", lambda e, yi=yi, cc=cc: e.scalar_tensor_tensor(
                    out=yb[yi][:, 0:2], in0=carry[par][:, cc, 0:2], scalar=V(V_FW + cc), in1=yb[yi][:, 0:2],
                    op0=ALU.mult, op1=ALU.add), reads=[carb[par][cc], ybf[yi]], writes=[ybf[yi]])
                S.add("dve", lambda e, yi=yi, cc=cc: e.scalar_tensor_tensor(
                    out=yb[yi][:, 0:1], in0=carry[par][:, cc, 1:2], scalar=V(V_FW + 88 + cc), in1=yb[yi][:, 0:1],
                    op0=ALU.mult, op1=ALU.add), reads=[carb[par][cc], ybf[yi]], writes=[ybf[yi]])
            si = rot("sb", 2)
            yg, yv = yis
            S.add("act", lambda e, yg=yg, si=si: e.activation(out=sbf[si][:, :], in_=yb[yg][:, :], func=AF.Silu),
                  reads=[ybf[yg]], writes=[sbb[si]])
            S.add("pool", lambda e, si=si, yv=yv, j=j: e.tensor_tensor(
                out=act[:, j, :], in0=sbf[si][:, :], in1=yb[yv][:, :], op=ALU.mult),
                reads=[sbb[si], ybf[yv]], writes=[actb[j]])
        if os.environ.get("MK_CUT", "") == "up":
            return
        host = (u + 1 < nunits) and u > 0 and os.environ.get("MK_NOHOST", "") == ""
        if host:
            x_load(u + 1, 0)
            x_load(u + 1, 1)
        for m in range(16):
            bk = next_bank("all")
            for t_ in range(3):
                nk = 16 if t_ < 2 else 12
                slot, slb = get_slab(sidx)
                sidx += 1
                for kk in range(nk):
                    k = 16 * t_ + kk
                    S.add("pe", lambda e, bk=bk, kk=kk, k=k, slot=slot: e.matmul(
                        banks[bk][:, :], lhsT=slot[:, kk * 128:(kk + 1) * 128], rhs=act[:, k, :],
                        start=(k == 0), stop=(k == NPAIR - 1)), reads=[slb, actb[k]], writes=[bankb[bk]])
            mi = m % 4
            S.add("dve", lambda e, bk=bk, m=m, mi=mi: e.tensor_tensor(
                out=oT[mi][:, :], in0=banks[bk][:, :], in1=xT[:, m, :], op=ALU.add),
                reads=[bankb[bk], xTb[m]], writes=[oTb[mi]])
            if mi == 3:
                for b in range(4):
                    bt = next_bank("all")
                    for mm in range(4):
                        S.add("pe", lambda e, bt=bt, mm=mm, b=b: e.transpose(
                            out=banks[bt][:, mm * 128:(mm + 1) * 128], in_=oT[mm][:, b * 128:(b + 1) * 128], identity=ident[:]),
                            reads=[oTb[mm]], writes=[bankb[bt]])
                    oi = rot("ost", 3)
                    S.add("act", lambda e, bt=bt, oi=oi: e.copy(out=ostage[oi][:, :], in_=banks[bt][:, :]),
                          reads=[bankb[bt]], writes=[ostb[oi]])
                    r_ = u * 512 + b * 128
                    c_ = (m - 3) * 128
                    if os.environ.get("MK_CUT", "") == "nostore":
                        continue
                    S.add("pool", lambda e, oi=oi, r_=r_, c_=c_: e.dma_start(
                        out=y_out[r_:r_ + 128, c_:c_ + 512], in_=ostage[oi][:, :]), reads=[ostb[oi]], key=f"ost{oi}")
                if host:
                    g_ = m // 4
                    head_group(u + 1, g_, "all")
                    if g_ + 2 < 4:
                        x_load(u + 1, g_ + 2)
        if host:
            head_finish(u + 1, "all")

    if nunits >= 0:
        mixer(-1)
    ALLE = ("pe", "act", "dve", "pool", "sp")
    for u in range(nunits):
        S.barrier(engines=ALLE if u == 1 else ALLE[:4])
        mixer(u)
        S.barrier(engines=ALLE if u == 0 else ALLE[:4])
        if os.environ.get("MK_CUT", "") == "mixer0":
            break
        ffn(u)

    keys = set(op.key for op in S.ops if op.key is not None)
    sems = {}
    for e in ("pe", "act", "dve", "pool"):
        sems[("e", e)] = nc.alloc_semaphore(f"s_{e}")
    for k in sorted(keys):
        sems[("k", k)] = nc.alloc_semaphore(f"k_{k}")
    nwait = S.emit(sems)
    return nc, len(S.ops), nwait


def _host_consts():
    c = np.zeros((128, 640), np.float32)
    c[:, 0:128] = np.eye(128, dtype=np.float32)
    c[:, 128:256] = 1.0
    c[0:64, 256:320] = 1.0
    c[64:128, 320:384] = 1.0
    kk = np.arange(128)[:, None]
    qq = np.arange(128)[None, :]
    if MASKMODE == "pe":
        NEG = -30000.0
        c[:, 384:512] = np.where(kk <= qq, 0.0, NEG)
        c[:, 512:640] = np.where(kk > qq, 0.0, NEG)
    else:
        c[:, 384:512] = (kk <= qq).astype(np.float32)
        c[:, 512:640] = (kk > qq).astype(np.float32)
    return c


def _host_vecs(inp):
    v = np.zeros((128, NV), np.float32)
    v[:, V_ANW:V_ANW + 16] = inp["attn_norm_w"][0].reshape(16, 128).T
    v[:, V_FNW:V_FNW + 16] = inp["ffn_norm_w"][0].reshape(16, 128).T
    v[:, V_QNW] = np.tile(inp["q_norm_w"][0], 2)
    v[:, V_KNW] = np.tile(inp["k_norm_w"][0], 2)
    cw = inp["conv_dw_w"][0]
    v[:, V_CW:V_CW + 248] = cw.reshape(CK, 8, 128).transpose(2, 1, 0).reshape(128, 248)
    v[:, V_CB:V_CB + 8] = inp["conv_dw_b"][0].reshape(8, 128).T
    v[:, V_LNW:V_LNW + 8] = inp["conv_ln_w"][0].reshape(8, 128).T
    v[:, V_LNB:V_LNB + 8] = inp["conv_ln_b"][0].reshape(8, 128).T
    fw = inp["ffn_dw_w"][0]
    v[:, V_FW:V_FW + 264] = fw.reshape(3, 88, 128).transpose(2, 0, 1).reshape(128, 264)
    v[:, V_FB:V_FB + 88] = inp["ffn_dw_b"][0].reshape(88, 128).T
    sk = inp["sinks"][0]
    for pp in range(2):
        for j in range(4):
            v[0:64, V_SINK + pp * 4 + j] = sk[8 * pp + j]
            v[64:128, V_SINK + pp * 4 + j] = sk[8 * pp + 4 + j]
    return v


_CACHE = {}


def kernel(**inputs):
    inp = {k: np.asarray(v) for k, v in inputs.items()}
    nunits = int(os.environ.get("MK_NUNITS", "8"))
    x = inp["x"].astype(np.float32, copy=False)
    if nunits not in _CACHE:
        _CACHE[nunits] = build_program(nunits)[0]
    nc = _CACHE[nunits]
    cst = _host_consts()
    vecs = _host_vecs(inp)
    w_in = np.ascontiguousarray(inp["w_in"][0])
    w_out = np.ascontiguousarray(inp["w_out"][0])
    w_up = np.ascontiguousarray(inp["w_up"][0])
    w_down = np.ascontiguousarray(inp["w_down"][0])
    in_maps = []
    for c in range(NCORES):
        b, q = divmod(c, 4)
        s0 = q * TOK
        xe = np.zeros((HALO + TOK, D), np.float32)
        if q > 0:
            xe[0:HALO] = x[b, s0 - HALO:s0]
        xe[HALO:] = x[b, s0:s0 + TOK]
        cc = np.zeros((128, 129), np.float32)
        cc[:, 0:128] = -30000.0 if MASKMODE == "pe" else 0.0
        if q > 0:
            cc[:, 0:128] = cst[:, 512:640]
            cc[:, 128] = 1.0
        in_maps.append({"x_ext": xe, "w_in": w_in, "w_out": w_out, "w_up": w_up, "w_down": w_down,
                        "vecs": vecs, "cst": cst, "ccore": cc})
    res = run_bass_kernel_spmd(nc, in_maps, core_ids=list(range(NCORES)))
    out = np.zeros((2, SEQ, D), np.float32)
    for c in range(NCORES):
        b, q = divmod(c, 4)
        out[b, q * TOK:(q + 1) * TOK] = res.results[c]["y_out"]
    return out
```

```python
import os
import numpy as np
import concourse.bass as bass
import concourse.mybir as mybir
from concourse.bass_utils import run_bass_kernel_spmd

F32 = mybir.dt.float32
BF16 = mybir.dt.bfloat16
AF = mybir.ActivationFunctionType
ALU = mybir.AluOpType

D = 2048
SEQ = 16384
NCORES = 8
TOK = 4096
HALO = 256
NH, NKV, DH = 16, 4, 64
QD, KVD = 1024, 256
CCH = 1024
CK = 31
INC = 3584
DFF = 5632
NPAIR = 44
EPS = 1e-6

IN0, OUT0, UP0, DN0 = 0, 28, 44, 132
NUNIT_SCR = 182
VW0 = 180

V_ANW, V_FNW, V_QNW, V_KNW = 0, 16, 32, 33
V_CW, V_CB, V_LNW, V_LNB = 34, 282, 290, 298
V_FW, V_FB, V_SINK = 306, 570, 658
NV = 666


class Buf:
    __slots__ = ("name", "w", "r")

    def __init__(self, name):
        self.name = name
        self.w = None
        self.r = []


class Op:
    __slots__ = ("eng", "fn", "deps", "inc", "cnt", "key")


class Sched:
    ENG = ("pe", "act", "dve", "pool", "sp")

    def __init__(self, nc):
        self.nc = nc
        self.ops = []
        self.pending = {e: set() for e in self.ENG}
        self.last = {e: None for e in self.ENG}
        self.lastkey = {}

    def add(self, eng, fn, reads=(), writes=(), key=None):
        op = Op()
        op.eng, op.fn, op.key, op.inc, op.cnt = eng, fn, key, False, 0
        idx = len(self.ops)
        deps = set()
        for b in reads:
            if b.w is not None:
                deps.add(b.w)
        for b in writes:
            if b.w is not None and not b.r:
                deps.add(b.w)
            lastr = {}
            for ri in b.r:
                p = self.ops[ri]
                if p.key is not None:
                    deps.add(ri)
                elif lastr.get(p.eng, -1) < ri:
                    lastr[p.eng] = ri
            deps.update(lastr.values())
        if self.pending[eng]:
            deps |= self.pending[eng]
            self.pending[eng] = set()
        real = set()
        for d in deps:
            p = self.ops[d]
            if eng == "pe" and p.eng == "pe" and p.key is None:
                continue
            real.add(d)
            if p.key is None:
                p.inc = True
        op.deps = real
        self.ops.append(op)
        for b in reads:
            b.r.append(idx)
        for b in writes:
            b.w = idx
            b.r = []
        if key is None:
            self.last[eng] = idx
        else:
            self.lastkey[key] = idx
        return idx

    def barrier(self, engines=("pe", "act", "dve", "pool")):
        deps = set()
        for e in ("pe", "act", "dve", "pool"):
            if self.last[e] is not None:
                deps.add(self.last[e])
                self.ops[self.last[e]].inc = True
        for k, i in self.lastkey.items():
            deps.add(i)
        for e in engines:
            self.pending[e] |= deps

    def emit(self, sems):
        nc = self.nc
        engs = {"pe": nc.tensor, "act": nc.scalar, "dve": nc.vector, "pool": nc.gpsimd, "sp": nc.sync}
        cnt = {e: 0 for e in self.ENG}
        kcnt = {}
        for op in self.ops:
            if op.key is not None:
                kcnt[op.key] = kcnt.get(op.key, 0) + 16
                op.cnt = kcnt[op.key]
            elif op.inc:
                cnt[op.eng] += 1
                op.cnt = cnt[op.eng]
        waited = {e: {} for e in self.ENG}
        nwait = 0
        for op in self.ops:
            eng = engs[op.eng]
            need = {}
            for d in op.deps:
                p = self.ops[d]
                s = ("k", p.key) if p.key is not None else ("e", p.eng)
                if need.get(s, 0) < p.cnt:
                    need[s] = p.cnt
            w = waited[op.eng]
            for s, v in need.items():
                if w.get(s, 0) < v:
                    eng.wait_ge(sems[s], v)
                    w[s] = v
                    nwait += 1
            if op.fn is None:
                continue
            ins = op.fn(eng)
            if op.key is not None:
                ins.then_inc(sems[("k", op.key)], 16)
            elif op.inc:
                ins.then_inc(sems[("e", op.eng)], 1)
        for k, v in kcnt.items():
            nc.sync.wait_ge(sems[("k", k)], v)
        return nwait


MASKMODE = os.environ.get("MK_MASK", "pe")
XQ = os.environ.get("MK_XQ", "act")


def build_program(nunits=8):
    nc = bass.Bass("TRN2", target_bir_lowering=False, dynamic_dma_scratch_size=2048)
    S = Sched(nc)

    x_ext = nc.dram_tensor("x_ext", [HALO + TOK, D], F32, kind="ExternalInput").ap()
    w_in = nc.dram_tensor("w_in", [D, INC], F32, kind="ExternalInput").ap()
    w_out = nc.dram_tensor("w_out", [D, D], F32, kind="ExternalInput").ap()
    w_up = nc.dram_tensor("w_up", [D, 2 * DFF], F32, kind="ExternalInput").ap()
    w_down = nc.dram_tensor("w_down", [DFF, D], F32, kind="ExternalInput").ap()
    vecs = nc.dram_tensor("vecs", [128, NV], F32, kind="ExternalInput").ap()
    cst = nc.dram_tensor("cst", [128, 640], F32, kind="ExternalInput").ap()
    ccore = nc.dram_tensor("ccore", [128, 129], F32, kind="ExternalInput").ap()
    y_out = nc.dram_tensor("y_out", [TOK, D], F32, kind="ExternalOutput").ap()
    wscr = nc.dram_tensor("wscr", [NUNIT_SCR, 128, 2048], BF16, kind="Internal").ap()

    off = [2048]

    def sb(name, shape, dt, at=None):
        nbytes = int(np.prod(shape[1:])) * (4 if dt == F32 else 2)
        if at is None:
            o = off[0]
            off[0] += (nbytes + 63) // 64 * 64
        else:
            o = at
        t = nc.alloc_sbuf_tensor_at(name, list(shape), dt, offset=o)
        return t, o, nbytes

    ident, _, _ = sb("ident", [128, 128], F32)
    ones_bf, _, _ = sb("ones_bf", [128, 128], BF16)
    blk_bf, _, _ = sb("blk_bf", [128, 128], BF16)
    maskC, _, _ = sb("maskC", [128, 4, 128], BF16)
    maskP, _, _ = sb("maskP", [128, 4, 128], BF16)
    maskP0, _, _ = sb("maskP0", [128, 4, 128], BF16)
    es = [sb(f"es{i}", [128, 4, 128], F32)[0] for i in range(2)]
    vec, _, _ = sb("vec", [128, NV], F32)
    hmask, _, _ = sb("hmask", [128, 1], F32)
    qsc, _, _ = sb("qsc", [128, 1], F32)
    essm, _, _ = sb("essm", [128, 8], F32)
    zer, _, _ = sb("zer", [128, 128], F32)
    xT, _, _ = sb("xT", [128, 16, 512], F32)
    hT, _, _ = sb("hT", [128, 16, 512], BF16)
    knT, _, _ = sb("knT", [128, 2, 640], BF16)
    Vr, _, _ = sb("Vr", [128, 5, 256], BF16)
    h2halo, _, _ = sb("h2halo", [128, 16, 2], BF16)
    carry = [sb(f"carry{i}", [128, 88, 2], F32)[0] for i in range(2)]
    ccarry, _, _ = sb("ccarry", [128, 8, 30], F32)
    NRING = 6
    wring = [sb(f"wring{i}", [128, 2048], BF16)[0] for i in range(NRING)]
    xs = [sb(f"xs{i}", [128, 2048], F32)[0] for i in range(2)]
    ident_bf, _, _ = sb("ident_bf", [128, 128], BF16)
    A0 = off[0]
    sq, _, _ = sb("sq", [128, 16, 512], BF16)
    stt = [sb(f"stt{i}", [128, 512], F32)[0] for i in range(4)]
    qnT, _, _ = sb("qnT", [128, 8, 512], BF16)
    sg = [sb(f"sg{i}", [128, 512], F32)[0] for i in range(2)]
    cbuf, _, _ = sb("cbuf", [128, 8, 544], F32)
    ybuf, _, _ = sb("ybuf", [128, 8, 512], F32)
    NE = 8
    Et = [sb(f"E{i}", [128, 4, 128], BF16)[0] for i in range(NE)]
    dtm = [sb(f"dtm{i}", [128, 512], F32)[0] for i in range(3)]
    A_END_M = off[0]
    off[0] = A0
    yb = [sb(f"yb{i}", [128, 512], F32)[0] for i in range(4)]
    sbf = [sb(f"sbf{i}", [128, 512], F32)[0] for i in range(2)]
    act, _, _ = sb("act", [128, NPAIR, 512], BF16)
    oT = [sb(f"oT{i}", [128, 512], F32)[0] for i in range(4)]
    ostage = [sb(f"ostage{i}", [128, 512], F32)[0] for i in range(3)]
    cvs = [sb(f"cvs{i}", [128, 1024], F32)[0] for i in range(3)]
    A_END_F = off[0]
    off[0] = A0
    pstage = [sb(f"pstage{i}", [128, 2048], F32)[0] for i in range(3)]
    pout = [sb(f"pout{i}", [128, 2048], BF16)[0] for i in range(3)]
    cstage, _, _ = sb("cstage", [128, 640], F32)
    ccstage, _, _ = sb("ccstage", [128, 129], F32)
    A_END_P = off[0]
    total = max(A_END_M, A_END_F, A_END_P)
    assert total <= 192 * 1024, total
    if os.environ.get("MK_VERBOSE"):
        print("SBUF: A0", A0, "mixer", A_END_M - A0, "ffn", A_END_F - A0, "pre", A_END_P - A0, "total", total)

    banks = [nc.alloc_psum_tensor(f"pb{i}", [128, 512], F32) for i in range(8)]
    bankb = [Buf(f"pb{i}") for i in range(8)]
    rr = {"main": 0, "all": 0}

    def next_bank(pool="main"):
        nb_ = 6 if pool == "main" else 7
        i = rr[pool] % nb_
        rr[pool] += 1
        return i

    xTb = [Buf(f"xT{k}") for k in range(16)]
    hTb = [Buf(f"hT{k}") for k in range(16)]
    sqb = [Buf(f"sq{k}") for k in range(16)]
    xsb = [Buf(f"xs{i}") for i in range(2)]
    kn_c = Buf("kn_c")
    kn_m = [Buf(f"kn_m{i}") for i in range(2)]
    vslot = [Buf(f"vs{i}") for i in range(5)]
    qnb = [Buf(f"qn{i}") for i in range(8)]
    cb_c = Buf("cb_c")
    ccb = Buf("ccarry")
    cb_m = [Buf(f"cb_m{i}") for i in range(8)]
    ybb = [Buf(f"ybuf{i}") for i in range(8)]
    sttb = [Buf(f"stt{i}") for i in range(4)]
    sgb = [Buf(f"sg{i}") for i in range(2)]
    Eb = [Buf(f"E{i}") for i in range(NE)]
    dtb = [Buf(f"dt{i}") for i in range(3)]
    h2hb = Buf("h2halo")
    carb = [[Buf(f"car{p}_{c}") for c in range(88)] for p in range(2)]
    ringb = [Buf(f"ring{i}") for i in range(NRING)]
    wvb = Buf("wv")
    ybf = [Buf(f"yb{i}") for i in range(4)]
    sbb = [Buf(f"sb{i}") for i in range(2)]
    actb = [Buf(f"act{j}") for j in range(NPAIR)]
    oTb = [Buf(f"oT{i}") for i in range(4)]
    ostb = [Buf(f"ost{i}") for i in range(3)]
    pstp = [[Buf(f"pst{i}_{l}") for l in range(6)] for i in range(3)]
    poutb = [Buf(f"pout{i}") for i in range(3)]
    cvsb = [Buf(f"cvs{i}") for i in range(3)]
    ctr = {"cvs": 0, "ring": 0, "stt": 0, "sg": 0, "E": 0, "dt": 0, "yb": 0, "sb": 0, "ost": 0, "pp": 0, "xs": 0}

    def rot(name, n):
        i = ctr[name] % n
        ctr[name] += 1
        return i

    cb_ = Buf("cstage")
    S.add("sp", lambda e: e.dma_start(out=cstage[:], in_=cst[:, :]), writes=[cb_], key="const0")
    ccb_ = Buf("ccstage")
    S.add("sp", lambda e: e.dma_start(out=ccstage[:], in_=ccore[:, :]), writes=[ccb_], key="const1")
    vecb = Buf("vec")
    S.add("sp", lambda e: e.dma_start(out=vec[:], in_=vecs[:, :]), writes=[vecb], key="const2")
    cbuf_ = Buf("consts")
    S.add("dve", lambda e: e.tensor_copy(out=ident[:], in_=cstage[:, 0:128]), reads=[cb_], writes=[cbuf_])
    S.add("dve", lambda e: e.tensor_copy(out=ones_bf[:], in_=cstage[:, 128:256]), reads=[cb_], writes=[cbuf_])
    S.add("dve", lambda e: e.tensor_copy(out=ident_bf[:], in_=cstage[:, 0:128]), reads=[cb_], writes=[cbuf_])
    S.add("dve", lambda e: e.tensor_copy(out=blk_bf[:], in_=cstage[:, 256:384]), reads=[cb_], writes=[cbuf_])
    for j in range(4):
        S.add("dve", lambda e, j=j: e.tensor_copy(out=maskC[:, j, :], in_=cstage[:, 384:512]), reads=[cb_], writes=[cbuf_])
        S.add("dve", lambda e, j=j: e.tensor_copy(out=maskP[:, j, :], in_=cstage[:, 512:640]), reads=[cb_], writes=[cbuf_])
        S.add("dve", lambda e, j=j: e.tensor_copy(out=maskP0[:, j, :], in_=ccstage[:, 0:128]), reads=[ccb_], writes=[cbuf_])
    S.add("dve", lambda e: e.tensor_copy(out=hmask[:], in_=ccstage[:, 128:129]), reads=[ccb_], writes=[cbuf_])
    S.add("dve", lambda e: e.memset(zer[:], 0.0), writes=[cbuf_])
    S.add("act", lambda e: e.activation(out=qsc[:], in_=vec[:, V_QNW:V_QNW + 1], func=AF.Identity, scale=0.125), reads=[vecb], writes=[cbuf_])
    S.add("act", lambda e: e.activation(out=essm[:], in_=vec[:, V_SINK:V_SINK + 8], func=AF.Exp), reads=[vecb], writes=[cbuf_])
    for pp in range(2):
        for j in range(4):
            S.add("dve", lambda e, pp=pp, j=j: e.tensor_scalar(
                out=es[pp][:, j, :], in0=zer[:], scalar1=essm[:, pp * 4 + j:pp * 4 + j + 1], scalar2=None, op0=ALU.add),
                reads=[cbuf_], writes=[cbuf_])
    S.add("pool", lambda e: e.memset(knT[:, :, 0:128], 0.0), writes=[kn_c])
    S.add("pool", lambda e: e.memset(Vr[:, 0, :], 0.0), writes=[vslot[0]])

    win_v = w_in.rearrange("(k p) c -> p k c", p=128)
    wup_v = w_up.rearrange("(k p) c -> p k c", p=128)
    wdn_v = w_down.rearrange("(k p) c -> p k c", p=128)
    woc_v = w_out[1024:2048, :].rearrange("(k p) c -> p k c", p=128)
    woa_v = w_out[0:1024, :].rearrange("(pp e j r) c -> e r pp j c", pp=2, e=2, j=4, r=64)
    cast_eng = ["act", "dve", "act", "dve", "pool"]

    def prepass_unit(loads, nk, dst):
        i = rot("pp", 3)
        for li, ld in enumerate(loads):
            S.add("sp", lambda e, ld=ld, i=i: ld(e, pstage[i]), writes=[pstp[i][li]], key=f"pin{i}")
        rd = pstp[i][0:len(loads)]
        ce = cast_eng[ctr["pp"] % len(cast_eng)]
        if ce == "act":
            S.add("act", lambda e, i=i: e.copy(out=pout[i][:, 0:nk * 128], in_=pstage[i][:, 0:nk * 128]),
                  reads=rd, writes=[poutb[i]])
        else:
            S.add(ce, lambda e, i=i: e.tensor_copy(out=pout[i][:, 0:nk * 128], in_=pstage[i][:, 0:nk * 128]),
                  reads=rd, writes=[poutb[i]])
        S.add("act", lambda e, i=i: e.dma_start(out=dst, in_=pout[i][:, 0:nk * 128]), reads=[poutb[i]], key=f"pout{i}")

    def st3(t, nk=16):
        return t[:, 0:nk * 128].rearrange("p (k c) -> p k c", c=128)

    in_order = []
    for i in range(8):
        in_order += [("a", i), ("g", i)]
    in_order += [("k", 0), ("k", 1)] + [("q", c) for c in range(8)]
    for s, (kind, i) in enumerate(in_order):
        if kind == "q":
            pp, j = divmod(i, 4)
            lds = []
            for e_ in range(2):
                head = 8 * pp + 4 * e_ + j
                lds.append(lambda e, t, head=head, e_=e_: e.dma_start(
                    out=st3(t)[:, :, e_ * 64:(e_ + 1) * 64], in_=win_v[:, :, head * 64:(head + 1) * 64]))
        else:
            c0_ = {"a": 1536, "g": 2560, "k": 1024}[kind] + i * 128
            lds = [lambda e, t, c0_=c0_: e.dma_start(out=st3(t), in_=win_v[:, :, c0_:c0_ + 128])]
        prepass_unit(lds, 16, wscr[IN0 + s])
    for h in range(2):
        lds = [lambda e, t, h=h: e.dma_start(out=t[:, :].rearrange("p (k c) -> p k c", c=256),
                                              in_=win_v[:, h * 8:(h + 1) * 8, 1280:1536])]
        prepass_unit(lds, 16, wscr[VW0 + h])
    for m in range(16):
        lds = []
        for e_ in range(2):
            for pp_ in range(2):
                lds.append(lambda e, t, m=m, e_=e_, pp_=pp_: e.dma_start(
                    out=st3(t)[e_ * 64:(e_ + 1) * 64, 4 * pp_:4 * pp_ + 4, :],
                    in_=woa_v[e_, :, pp_, :, m * 128:(m + 1) * 128]))
        lds.append(lambda e, t, m=m: e.dma_start(out=st3(t)[:, 8:16, :], in_=woc_v[:, :, m * 128:(m + 1) * 128]))
        prepass_unit(lds, 16, wscr[OUT0 + m])
    S.barrier(engines=("pe", "act", "dve", "pool", "sp"))

    def load_unit(scr_idx, ncols=2048):
        i = rot("ring", NRING)
        S.add("sp", lambda e, i=i: e.dma_start(out=wring[i][:, 0:ncols], in_=wscr[scr_idx][:, 0:ncols]),
              writes=[ringb[i]], key=f"ring{i}")
        return wring[i], ringb[i]

    def load_unit_cvt(scr_idx, nk, src):
        i = rot("ring", NRING)
        for h in range(2):
            k0, k1 = 8 * h, min(nk, 8 * h + 8)
            if k1 <= k0:
                continue
            si = rot("cvs", 3)
            if src[0] == "up":
                ap = wup_v[:, k0:k1, src[1] * 128:(src[1] + 1) * 128]
            else:
                ap = wdn_v[:, 16 * src[2] + k0:16 * src[2] + k1, src[1] * 128:(src[1] + 1) * 128]
            S.add("sp", lambda e, si=si, ap=ap, k0=k0, k1=k1: e.dma_start(
                out=cvs[si][:, 0:(k1 - k0) * 128].rearrange("p (k c) -> p k c", c=128), in_=ap),
                writes=[cvsb[si]], key=f"cvs{si}")
            S.add("act", lambda e, si=si, i=i, k0=k0, k1=k1: e.copy(
                out=wring[i][:, k0 * 128:k1 * 128], in_=cvs[si][:, 0:(k1 - k0) * 128]),
                reads=[cvsb[si]], writes=[ringb[i]])
        S.add("act", lambda e, i=i: e.dma_start(out=wscr[scr_idx][:, 0:nk * 128], in_=wring[i][:, 0:nk * 128]),
              reads=[ringb[i]], key=f"rst{i}")
        return wring[i], ringb[i]

    def V(c, n=1):
        return vec[:, c:c + n]

    conv_state = {"glu": 0, "gen": None}

    xslots = {}
    head_done = {}

    def unit_geo(u):
        n = 256 if u < 0 else 512
        r0 = 0 if u < 0 else HALO + 512 * u
        return n, n // 128, r0

    def x_load(u, g):
        n, nb, r0 = unit_geo(u)
        xi = rot("xs", 2)
        S.add(XQ, lambda e: e.dma_start(
            out=xs[xi][:, 0:nb * 512].rearrange("p (b c) -> p b c", c=512),
            in_=x_ext[r0:r0 + n, 512 * g:512 * g + 512].rearrange("(b p) c -> p b c", p=128)),
            writes=[xsb[xi]], key=f"xs{xi}")
        xslots[(u, g)] = xi

    def head_group(u, g, pool):
        n, nb, r0 = unit_geo(u)
        xi = xslots.pop((u, g))
        for kk in range(4):
            k = 4 * g + kk
            bk = next_bank(pool)
            for b in range(nb):
                S.add("pe", lambda e, bk=bk, kk=kk, b=b: e.transpose(
                    out=banks[bk][:, b * 128:(b + 1) * 128], in_=xs[xi][:, b * 512 + kk * 128:b * 512 + (kk + 1) * 128],
                    identity=ident[:]), reads=[xsb[xi]], writes=[bankb[bk]])
            S.add("act", lambda e, bk=bk, k=k: e.copy(out=xT[:, k, 0:n], in_=banks[bk][:, 0:n]),
                  reads=[bankb[bk]], writes=[xTb[k]])
            S.add("act", lambda e, bk=bk, k=k: e.activation(out=sq[:, k, 0:n], in_=banks[bk][:, 0:n], func=AF.Square),
                  reads=[bankb[bk]], writes=[sqb[k]])

    def head_finish(u, pool):
        n, nb, r0 = unit_geo(u)
        bs = next_bank(pool)
        for k in range(16):
            S.add("pe", lambda e, k=k: e.matmul(banks[bs][:, 0:n], lhsT=ones_bf[:], rhs=sq[:, k, 0:n],
                                                start=(k == 0), stop=(k == 15)),
                  reads=[sqb[k]], writes=[bankb[bs]])
        t1 = rot("stt", 4)
        S.add("act", lambda e: e.activation(out=stt[t1][:, 0:n], in_=banks[bs][:, 0:n], func=AF.Ln,
                                            scale=1.0 / D, bias=EPS), reads=[bankb[bs]], writes=[sttb[t1]])
        S.add("act", lambda e: e.activation(out=banks[bs][:, 0:n], in_=stt[t1][:, 0:n], func=AF.Exp, scale=-0.5),
              reads=[sttb[t1]], writes=[bankb[bs]])
        for k in range(16):
            S.add("dve", lambda e, k=k: e.scalar_tensor_tensor(
                out=hT[:, k, 0:n], in0=xT[:, k, 0:n], scalar=V(V_ANW + k), in1=banks[bs][:, 0:n],
                op0=ALU.mult, op1=ALU.mult), reads=[xTb[k], bankb[bs]], writes=[hTb[k]])
        head_done[u] = True

    def head_inline(u):
        x_load(u, 0)
        x_load(u, 1)
        for g in range(4):
            head_group(u, g, "main")
            if g + 2 < 4:
                x_load(u, g + 2)
        head_finish(u, "main")

    def mixer(u):
        n = 256 if u < 0 else 512
        c0 = 128 if u < 0 else 0
        w = n - c0
        nb = n // 128
        r0 = 0 if u < 0 else HALO + 512 * u
        if u >= 0:
            pn = 256 if u == 0 else 512
            pnb = pn // 128
            S.add("pool", lambda e: e.tensor_copy(out=knT[:, :, 0:128], in_=knT[:, :, pn:pn + 128]),
                  reads=kn_m, writes=[kn_c])
            S.add("pool", lambda e: e.tensor_copy(out=Vr[:, 0, :], in_=Vr[:, pnb, :]), reads=[vslot[pnb]], writes=[vslot[0]])
            S.add("pool", lambda e: e.tensor_copy(out=cbuf[:, :, 0:30], in_=ccarry[:, :, :]), reads=[ccb], writes=[cb_c])
        else:
            S.add("pool", lambda e: e.memset(cbuf[:, :, 0:30], 0.0), writes=[cb_c])
        if not head_done.get(u):
            head_inline(u)

        NPT = int(os.environ.get("MK_NPT", "10"))

        def conv_gen():
            for i in range(8):
                while conv_state["glu"] <= i:
                    yield False
                cvb = 6 + (i % 2)
                for k in range(NPT):
                    if k == 0:
                        S.add("pool", lambda e, i=i, k=k: e.tensor_scalar(
                            out=ybuf[:, i, c0:n], in0=cbuf[:, i, c0 + k:n + k], scalar1=V(V_CW + i * 31 + k), scalar2=0.0,
                            op0=ALU.mult, op1=ALU.add), reads=[cb_m[i], cb_c], writes=[ybb[i]])
                    else:
                        S.add("pool", lambda e, i=i, k=k: e.tensor_scalar(
                            out=dtm[2][:, 0:w], in0=cbuf[:, i, c0 + k:n + k], scalar1=V(V_CW + i * 31 + k), scalar2=0.0,
                            op0=ALU.mult, op1=ALU.add), reads=[cb_m[i], cb_c], writes=[dtb[2]])
                        S.add("pool", lambda e, i=i: e.tensor_tensor(
                            out=ybuf[:, i, c0:n], in0=ybuf[:, i, c0:n], in1=dtm[2][:, 0:w], op=ALU.add),
                            reads=[dtb[2], ybb[i]], writes=[ybb[i]])
                S.add("act", lambda e, i=i, cvb=cvb: e.activation(
                    out=banks[cvb][:, 0:w], in_=cbuf[:, i, c0 + 30:n + 30], func=AF.Identity,
                    scale=V(V_CW + i * 31 + 30), bias=V(V_CB + i)),
                    reads=[cb_m[i]], writes=[bankb[cvb]])
                yield True
                for k in range(29, NPT - 1, -1):
                    S.add("dve", lambda e, i=i, k=k, cvb=cvb: e.scalar_tensor_tensor(
                        out=banks[cvb][:, 0:w], in0=cbuf[:, i, c0 + k:n + k], scalar=V(V_CW + i * 31 + k),
                        in1=banks[cvb][:, 0:w], op0=ALU.mult, op1=ALU.add),
                        reads=[cb_m[i], cb_c], writes=[bankb[cvb]])
                    yield True
                def fin(i=i, cvb=cvb):
                    S.add("dve", lambda e: e.tensor_tensor(
                        out=ybuf[:, i, c0:n], in0=banks[cvb][:, 0:w], in1=ybuf[:, i, c0:n], op=ALU.add),
                        reads=[bankb[cvb], ybb[i]], writes=[ybb[i]])
                    S.add("act", lambda e: e.copy(out=sq[:, i, c0:n], in_=ybuf[:, i, c0:n]),
                          reads=[ybb[i]], writes=[sqb[i]])
                    S.add("act", lambda e: e.activation(out=sq[:, 8 + i, c0:n], in_=ybuf[:, i, c0:n], func=AF.Square),
                          reads=[ybb[i]], writes=[sqb[8 + i]])
                if pend_fin:
                    pend_fin.pop()()
                pend_fin.append(fin)
                yield True
            while pend_fin:
                pend_fin.pop()()
            yield True

        pend_fin = []
        conv_state["glu"] = 0
        cg = conv_gen()
        cdone = [False]

        def conv_adv(k):
            if cdone[0]:
                return
            for _ in range(k):
                try:
                    r = next(cg)
                except StopIteration:
                    cdone[0] = True
                    return
                if r is False:
                    return

        deferred = []
        pend_a = {}
        for s, (kind, i) in enumerate(in_order):
            slot, slb = load_unit(IN0 + s)
            bk = next_bank()
            for k in range(16):
                S.add("pe", lambda e, bk=bk, k=k, slot=slot: e.matmul(
                    banks[bk][:, 0:n], lhsT=slot[:, k * 128:(k + 1) * 128], rhs=hT[:, k, 0:n],
                    start=(k == 0), stop=(k == 15)), reads=[slb, hTb[k]], writes=[bankb[bk]])
            for f in deferred:
                f()
            deferred = []
            if kind == "a":
                pend_a[i] = bk
            elif kind == "g":
                ba = pend_a.pop(i)
                sgi = rot("sg", 2)
                S.add("act", lambda e, bk=bk, sgi=sgi: e.activation(out=sg[sgi][:, 0:n], in_=banks[bk][:, 0:n],
                                                                   func=AF.Sigmoid),
                      reads=[bankb[bk]], writes=[sgb[sgi]])
                S.add("dve", lambda e, ba=ba, sgi=sgi, i=i: e.tensor_tensor(
                    out=cbuf[:, i, 30:30 + n], in0=banks[ba][:, 0:n], in1=sg[sgi][:, 0:n], op=ALU.mult),
                    reads=[bankb[ba], sgb[sgi]], writes=[cb_m[i]])
                conv_state["glu"] = i + 1
            else:
                ei = rot("E", NE)
                sqq = Et[ei][:, :, :].rearrange("p a c -> p (a c)")
                S.add("act", lambda e, bk=bk, sqq=sqq: e.activation(out=sqq[:, 0:n], in_=banks[bk][:, 0:n], func=AF.Square),
                      reads=[bankb[bk]], writes=[Eb[ei]])

                def post(bk=bk, ei=ei, sqq=sqq, kind=kind, i=i):
                    b2 = next_bank()
                    S.add("pe", lambda e: e.matmul(banks[b2][:, 0:n], lhsT=blk_bf[:], rhs=sqq[:, 0:n], start=True, stop=True),
                          reads=[Eb[ei]], writes=[bankb[b2]])
                    t2 = rot("stt", 4)
                    S.add("act", lambda e: e.activation(out=stt[t2][:, 0:n], in_=banks[b2][:, 0:n], func=AF.Ln,
                                                        scale=1.0 / DH, bias=EPS), reads=[bankb[b2]], writes=[sttb[t2]])
                    S.add("act", lambda e: e.activation(out=stt[t2][:, 0:n], in_=stt[t2][:, 0:n], func=AF.Exp, scale=-0.5),
                          reads=[sttb[t2]], writes=[sttb[t2]])
                    if kind == "q":
                        S.add("dve", lambda e: e.scalar_tensor_tensor(
                            out=qnT[:, i, 0:n], in0=banks[bk][:, 0:n], scalar=qsc[:, 0:1], in1=stt[t2][:, 0:n],
                            op0=ALU.mult, op1=ALU.mult), reads=[bankb[bk], sttb[t2]], writes=[qnb[i]])
                    else:
                        S.add("dve", lambda e: e.scalar_tensor_tensor(
                            out=knT[:, i, 128:128 + n], in0=banks[bk][:, 0:n], scalar=V(V_KNW), in1=stt[t2][:, 0:n],
                            op0=ALU.mult, op1=ALU.mult), reads=[bankb[bk], sttb[t2]], writes=[kn_m[i]])
                deferred.append(post)
            conv_adv(6)
        vsl = [load_unit(VW0 + h_) for h_ in range(2)]
        for b in range(nb):
            bk = next_bank()
            for k in range(16):
                h_, kk_ = divmod(k, 8)
                S.add("pe", lambda e, bk=bk, k=k, b=b, h_=h_, kk_=kk_: e.matmul(
                    banks[bk][:, 0:256], lhsT=hT[:, k, b * 128:(b + 1) * 128],
                    rhs=vsl[h_][0][:, kk_ * 256:(kk_ + 1) * 256], start=(k == 0), stop=(k == 15)),
                    reads=[vsl[h_][1], hTb[k]], writes=[bankb[bk]])
            if b == 0:
                for f in deferred:
                    f()
                deferred = []
            S.add("act", lambda e, bk=bk, b=b: e.copy(out=Vr[:, b + 1, :], in_=banks[bk][:, 0:256]),
                  reads=[bankb[bk]], writes=[vslot[b + 1]])
            conv_adv(4)

        S.add("pool", lambda e: e.tensor_copy(out=ccarry[:, :, :], in_=cbuf[:, :, n:n + 30]), reads=cb_m, writes=[ccb])
        def att_phase1(qb, pp):
            items = []
            for e_ in range(2):
                g = 2 * pp + e_
                for kbi in range(2):
                    kcol = qb * 128 + kbi * 128
                    vs_ = qb + kbi
                    if kbi == 0:
                        mk = maskP0 if (u == 0 and qb == 0) else maskP
                        knr = [kn_c] if qb == 0 else [kn_m[pp]]
                    else:
                        mk = maskC
                        knr = [kn_m[pp]]
                    sbk = next_bank()
                    ei = rot("E", NE)
                    S.add("pe", lambda e, sbk=sbk, e_=e_, pp=pp, kcol=kcol, qb=qb: e.matmul(
                        banks[sbk][:, :].rearrange("p (a c) -> p a c", c=128),
                        lhsT=knT[e_ * 64:(e_ + 1) * 64, pp, kcol:kcol + 128],
                        rhs=qnT[e_ * 64:(e_ + 1) * 64, 4 * pp:4 * pp + 4, qb * 128:(qb + 1) * 128],
                        start=True, stop=(MASKMODE != "pe")),
                        reads=knr + [qnb[4 * pp + j] for j in range(4)], writes=[bankb[sbk]])
                    if MASKMODE == "pe":
                        S.add("pe", lambda e, sbk=sbk, mk=mk: e.matmul(
                            banks[sbk][:, :].rearrange("p (a c) -> p a c", c=128), lhsT=ident_bf[:], rhs=mk[:, :, :],
                            start=False, stop=True), reads=[], writes=[bankb[sbk]])
                    S.add("act", lambda e, sbk=sbk, ei=ei: e.activation(
                        out=Et[ei][:, :, :], in_=banks[sbk][:, :].rearrange("p (a c) -> p a c", c=128), func=AF.Exp),
                        reads=[bankb[sbk]], writes=[Eb[ei]])
                    if MASKMODE != "pe":
                        S.add("pool" if MASKMODE == "pool" else "dve", lambda e, ei=ei, mk=mk: e.tensor_tensor(
                            out=Et[ei][:, :, :], in0=Et[ei][:, :, :], in1=mk[:, :, :], op=ALU.mult),
                            reads=[Eb[ei]], writes=[Eb[ei]])
                    items.append((e_, g, kbi, vs_, ei))
            return items

        def att_phase2(qb, pp, items):
            pvb = next_bank()
            smb = next_bank()
            for (e_, g, kbi, vs_, ei) in items:
                S.add("pe", lambda e, pvb=pvb, e_=e_, g=g, vs_=vs_, ei=ei, kbi=kbi: e.matmul(
                    banks[pvb][e_ * 64:(e_ + 1) * 64, :], lhsT=Vr[:, vs_, g * 64:(g + 1) * 64],
                    rhs=Et[ei][:, :, :].rearrange("p a c -> p (a c)"), start=(kbi == 0), stop=(kbi == 1)),
                    reads=[vslot[vs_], Eb[ei]], writes=[bankb[pvb]])
                S.add("pe", lambda e, smb=smb, e_=e_, ei=ei, kbi=kbi: e.matmul(
                    banks[smb][e_ * 64:(e_ + 1) * 64, :], lhsT=ones_bf[:, 0:64],
                    rhs=Et[ei][:, :, :].rearrange("p a c -> p (a c)"), start=(kbi == 0), stop=(kbi == 1)),
                    reads=[Eb[ei]], writes=[bankb[smb]])
            di = rot("dt", 2)
            S.add("dve", lambda e, smb=smb, di=di, pp=pp: e.tensor_tensor(
                out=dtm[di][:, :], in0=banks[smb][:, :], in1=es[pp][:, :, :].rearrange("p a c -> p (a c)"), op=ALU.add),
                reads=[bankb[smb]], writes=[dtb[di]])
            S.add("act", lambda e, di=di: e.activation(out=dtm[di][:, :], in_=dtm[di][:, :], func=AF.Ln),
                  reads=[dtb[di]], writes=[dtb[di]])
            S.add("act", lambda e, di=di: e.activation(out=dtm[di][:, :], in_=dtm[di][:, :], func=AF.Exp, scale=-1.0),
                  reads=[dtb[di]], writes=[dtb[di]])
            S.add("dve", lambda e, pvb=pvb, di=di, pp=pp, qb=qb: e.tensor_tensor(
                out=hT[:, 4 * pp:4 * pp + 4, qb * 128:(qb + 1) * 128],
                in0=banks[pvb][:, :].rearrange("p (a c) -> p a c", c=128),
                in1=dtm[di][:, :].rearrange("p (a c) -> p a c", c=128), op=ALU.mult),
                reads=[bankb[pvb], dtb[di]], writes=[hTb[4 * pp + j] for j in range(4)])
            conv_adv(8)

        prev_step = None
        for qb in range(c0 // 128, nb):
            for pp in range(2):
                items = att_phase1(qb, pp)
                if prev_step is not None:
                    att_phase2(*prev_step)
                prev_step = (qb, pp, items)
        att_phase2(*prev_step)
        while not cdone[0]:
            conv_adv(1000)

        bM = next_bank()
        bQ = next_bank()
        for i in range(8):
            S.add("pe", lambda e, i=i: e.matmul(banks[bM][:, 0:w], lhsT=ones_bf[:], rhs=sq[:, i, c0:n],
                                                start=(i == 0), stop=(i == 7)), reads=[sqb[i]], writes=[bankb[bM]])
        for i in range(8):
            S.add("pe", lambda e, i=i: e.matmul(banks[bQ][:, 0:w], lhsT=ones_bf[:], rhs=sq[:, 8 + i, c0:n],
                                                start=(i == 0), stop=(i == 7)), reads=[sqb[8 + i]], writes=[bankb[bQ]])
        t3 = rot("stt", 4)
        S.add("act", lambda e: e.activation(out=stt[t3][:, 0:w], in_=banks[bM][:, 0:w], func=AF.Square, scale=1.0 / CCH),
              reads=[bankb[bM]], writes=[sttb[t3]])
        S.add("dve", lambda e: e.scalar_tensor_tensor(
            out=stt[t3][:, 0:w], in0=banks[bQ][:, 0:w], scalar=1.0 / CCH, in1=stt[t3][:, 0:w],
            op0=ALU.mult, op1=ALU.subtract), reads=[bankb[bQ], sttb[t3]], writes=[sttb[t3]])
        S.add("act", lambda e: e.activation(out=stt[t3][:, 0:w], in_=stt[t3][:, 0:w], func=AF.Ln, bias=EPS),
              reads=[sttb[t3]], writes=[sttb[t3]])
        S.add("act", lambda e: e.activation(out=banks[bQ][:, 0:w], in_=stt[t3][:, 0:w], func=AF.Exp, scale=-0.5),
              reads=[sttb[t3]], writes=[bankb[bQ]])
        for i in range(8):
            d1 = rot("dt", 2)
            S.add("dve", lambda e, i=i, d1=d1: e.scalar_tensor_tensor(
                out=dtm[d1][:, 0:w], in0=banks[bM][:, 0:w], scalar=-1.0 / CCH, in1=ybuf[:, i, c0:n],
                op0=ALU.mult, op1=ALU.add), reads=[bankb[bM], ybb[i]], writes=[dtb[d1]])
            S.add("dve", lambda e, d1=d1: e.tensor_tensor(
                out=dtm[d1][:, 0:w], in0=dtm[d1][:, 0:w], in1=banks[bQ][:, 0:w], op=ALU.mult),
                reads=[dtb[d1], bankb[bQ]], writes=[dtb[d1]])
            S.add("act", lambda e, i=i, d1=d1: e.activation(
                out=hT[:, 8 + i, c0:n], in_=dtm[d1][:, 0:w], func=AF.Silu, scale=V(V_LNW + i), bias=V(V_LNB + i)),
                reads=[dtb[d1]], writes=[hTb[8 + i]])

        for m in range(16):
            slot, slb = load_unit(OUT0 + m)
            bk = next_bank()
            for k in range(16):
                S.add("pe", lambda e, bk=bk, k=k, slot=slot: e.matmul(
                    banks[bk][:, 0:w], lhsT=slot[:, k * 128:(k + 1) * 128], rhs=hT[:, k, c0:n],
                    start=(k == 0), stop=(k == 15)), reads=[slb, hTb[k]], writes=[bankb[bk]])
            S.add("dve", lambda e, bk=bk, m=m: e.tensor_tensor(
                out=xT[:, m, c0:n], in0=banks[bk][:, 0:w], in1=xT[:, m, c0:n], op=ALU.add),
                reads=[bankb[bk], xTb[m]], writes=[xTb[m]])
            S.add("act", lambda e, m=m: e.activation(out=sq[:, m, c0:n], in_=xT[:, m, c0:n], func=AF.Square),
                  reads=[xTb[m]], writes=[sqb[m]])
        bs2 = next_bank()
        for m in range(16):
            S.add("pe", lambda e, m=m: e.matmul(banks[bs2][:, 0:w], lhsT=ones_bf[:], rhs=sq[:, m, c0:n],
                                                start=(m == 0), stop=(m == 15)), reads=[sqb[m]], writes=[bankb[bs2]])
        t4 = rot("stt", 4)
        S.add("act", lambda e: e.activation(out=stt[t4][:, 0:w], in_=banks[bs2][:, 0:w], func=AF.Ln,
                                            scale=1.0 / D, bias=EPS), reads=[bankb[bs2]], writes=[sttb[t4]])
        S.add("act", lambda e: e.activation(out=banks[bs2][:, 0:w], in_=stt[t4][:, 0:w], func=AF.Exp, scale=-0.5),
              reads=[sttb[t4]], writes=[bankb[bs2]])
        for m in range(16):
            S.add("dve", lambda e, m=m: e.scalar_tensor_tensor(
                out=hT[:, m, c0:n], in0=xT[:, m, c0:n], scalar=V(V_FNW + m), in1=banks[bs2][:, 0:w],
                op0=ALU.mult, op1=ALU.mult), reads=[xTb[m], bankb[bs2]], writes=[hTb[m]])
        if u < 0:
            S.add("pool", lambda e: e.tensor_copy(out=h2halo[:, :, :], in_=hT[:, :, n - 2:n]), reads=hTb, writes=[h2hb])
            pass

    HB = 7

    def ffn(u):
        par = u % 2
        order = []
        for j in range(NPAIR):
            for cc in (j, NPAIR + j):
                order.append((UP0 + cc, 16, ("up", cc)))
        for m in range(16):
            for t_ in range(3):
                order.append((DN0 + 3 * m + t_, 16 if t_ < 2 else 12, ("dn", m, t_)))
        slots = {}
        nxt = [0]
        look = 3 if u == 0 else 0

        def get_slab(si_):
            while nxt[0] <= min(si_ + look, len(order) - 1):
                scr, nk_, src = order[nxt[0]]
                slots[nxt[0]] = load_unit_cvt(scr, nk_, src) if u == 0 else load_unit(scr, nk_ * 128)
                nxt[0] += 1
            return slots.pop(si_)

        sidx = 0
        for j in range(NPAIR):
            yis = []
            for cc in (j, NPAIR + j):
                slot, slb = get_slab(sidx)
                sidx += 1
                bk = next_bank("all")
                for k in range(16):
                    S.add("pe", lambda e, bk=bk, k=k, slot=slot: e.matmul(
                        banks[bk][:, :], lhsT=slot[:, k * 128:(k + 1) * 128], rhs=hT[:, k, :],
                        start=(k == 0), stop=(k == 15)), reads=[slb, hTb[k]], writes=[bankb[bk]])
                if u == 0:
                    for k in range(16):
                        S.add("pe", lambda e, k=k, slot=slot, cc=cc: e.matmul(
                            banks[HB][:, 2 * cc:2 * cc + 2], lhsT=slot[:, k * 128:(k + 1) * 128], rhs=h2halo[:, k, :],
                            start=(k == 0), stop=(k == 15)), reads=[slb, h2hb], writes=[bankb[HB]])
                    S.add("act", lambda e, cc=cc: e.activation(
                        out=carry[par][:, cc, :], in_=banks[HB][:, 2 * cc:2 * cc + 2], func=AF.Identity, scale=hmask[:, 0:1]),
                        reads=[bankb[HB]], writes=[carb[par][cc]])
                yi = rot("yb", 4)
                yis.append(yi)
                S.add("act", lambda e, bk=bk, yi=yi, cc=cc: e.activation(
                    out=yb[yi][:, :], in_=banks[bk][:, :], func=AF.Identity, scale=V(V_FW + 2 * 88 + cc), bias=V(V_FB + cc)),
                    reads=[bankb[bk]], writes=[ybf[yi]])
                S.add("act", lambda e, bk=bk, cc=cc: e.copy(out=carry[1 - par][:, cc, :], in_=banks[bk][:, 510:512]),
                      reads=[bankb[bk]], writes=[carb[1 - par][cc]])
                S.add("dve", lambda e, bk=bk, yi=yi, cc=cc: e.scalar_tensor_tensor(
                    out=yb[yi][:, 1:512], in0=banks[bk][:, 0:511], scalar=V(V_FW + 88 + cc), in1=yb[yi][:, 1:512],
                    op0=ALU.mult, op1=ALU.add), reads=[bankb[bk], ybf[yi]], writes=[ybf[yi]])
                S.add("dve", lambda e, bk=bk, yi=yi, cc=cc: e.scalar_tensor_tensor(
                    out=yb[yi][:, 2:512], in0=banks[bk][:, 0:510], scalar=V(V_FW + cc), in1=yb[yi][:, 2:512],
                    op0=ALU.mult, op1=ALU.add), reads=[bankb[bk], ybf[yi]], writes=[ybf[yi]])
                S.add("dve", lambda e, yi=yi, cc=cc: e.scalar_tensor_tensor(
                    out=yb[yi][:, 0:2], in0=carry[par][:, cc, 0:2], scalar=V(V_FW + cc), in1=yb[yi][:, 0:2],
                    op0=ALU.mult, op1=ALU.add), reads=[carb[par][cc], ybf[yi]], writes=[ybf[yi]])
                S.add("dve", lambda e, yi=yi, cc=cc: e.scalar_tensor_tensor(
                    out=yb[yi][:, 0:1], in0=carry[par][:, cc, 1:2], scalar=V(V_FW + 88 + cc), in1=yb[yi][:, 0:1],
                    op0=ALU.mult, op1=ALU.add), reads=[carb[par][cc], ybf[yi]], writes=[ybf[yi]])
            si = rot("sb", 2)
            yg, yv = yis
            S.add("act", lambda e, yg=yg, si=si: e.activation(out=sbf[si][:, :], in_=yb[yg][:, :], func=AF.Silu),
                  reads=[ybf[yg]], writes=[sbb[si]])
            S.add("pool", lambda e, si=si, yv=yv, j=j: e.tensor_tensor(
                out=act[:, j, :], in0=sbf[si][:, :], in1=yb[yv][:, :], op=ALU.mult),
                reads=[sbb[si], ybf[yv]], writes=[actb[j]])
        if os.environ.get("MK_CUT", "") == "up":
            return
        host = (u + 1 < nunits) and u > 0 and os.environ.get("MK_NOHOST", "") == ""
        if host:
            x_load(u + 1, 0)
            x_load(u + 1, 1)
        for m in range(16):
            bk = next_bank("all")
            for t_ in range(3):
                nk = 16 if t_ < 2 else 12
                slot, slb = get_slab(sidx)
                sidx += 1
                for kk in range(nk):
                    k = 16 * t_ + kk
                    S.add("pe", lambda e, bk=bk, kk=kk, k=k, slot=slot: e.matmul(
                        banks[bk][:, :], lhsT=slot[:, kk * 128:(kk + 1) * 128], rhs=act[:, k, :],
                        start=(k == 0), stop=(k == NPAIR - 1)), reads=[slb, actb[k]], writes=[bankb[bk]])
            mi = m % 4
            S.add("dve", lambda e, bk=bk, m=m, mi=mi: e.tensor_tensor(
                out=oT[mi][:, :], in0=banks[bk][:, :], in1=xT[:, m, :], op=ALU.add),
                reads=[bankb[bk], xTb[m]], writes=[oTb[mi]])
            if mi == 3:
                for b in range(4):
                    bt = next_bank("all")
                    for mm in range(4):
                        S.add("pe", lambda e, bt=bt, mm=mm, b=b: e.transpose(
                            out=banks[bt][:, mm * 128:(mm + 1) * 128], in_=oT[mm][:, b * 128:(b + 1) * 128], identity=ident[:]),
                            reads=[oTb[mm]], writes=[bankb[bt]])
                    oi = rot("ost", 3)
                    S.add("act", lambda e, bt=bt, oi=oi: e.copy(out=ostage[oi][:, :], in_=banks[bt][:, :]),
                          reads=[bankb[bt]], writes=[ostb[oi]])
                    r_ = u * 512 + b * 128
                    c_ = (m - 3) * 128
                    if os.environ.get("MK_CUT", "") == "nostore":
                        continue
                    S.add("pool", lambda e, oi=oi, r_=r_, c_=c_: e.dma_start(
                        out=y_out[r_:r_ + 128, c_:c_ + 512], in_=ostage[oi][:, :]), reads=[ostb[oi]], key=f"ost{oi}")
                if host:
                    g_ = m // 4
                    head_group(u + 1, g_, "all")
                    if g_ + 2 < 4:
                        x_load(u + 1, g_ + 2)
        if host:
            head_finish(u + 1, "all")

    if nunits >= 0:
        mixer(-1)
    ALLE = ("pe", "act", "dve", "pool", "sp")
    for u in range(nunits):
        S.barrier(engines=ALLE if u == 1 else ALLE[:4])
        mixer(u)
        S.barrier(engines=ALLE if u == 0 else ALLE[:4])
        if os.environ.get("MK_CUT", "") == "mixer0":
            break
        ffn(u)

    keys = set(op.key for op in S.ops if op.key is not None)
    sems = {}
    for e in ("pe", "act", "dve", "pool"):
        sems[("e", e)] = nc.alloc_semaphore(f"s_{e}")
    for k in sorted(keys):
        sems[("k", k)] = nc.alloc_semaphore(f"k_{k}")
    nwait = S.emit(sems)
    return nc, len(S.ops), nwait


def _host_consts():
    c = np.zeros((128, 640), np.float32)
    c[:, 0:128] = np.eye(128, dtype=np.float32)
    c[:, 128:256] = 1.0
    c[0:64, 256:320] = 1.0
    c[64:128, 320:384] = 1.0
    kk = np.arange(128)[:, None]
    qq = np.arange(128)[None, :]
    if MASKMODE == "pe":
        NEG = -30000.0
        c[:, 384:512] = np.where(kk <= qq, 0.0, NEG)
        c[:, 512:640] = np.where(kk > qq, 0.0, NEG)
    else:
        c[:, 384:512] = (kk <= qq).astype(np.float32)
        c[:, 512:640] = (kk > qq).astype(np.float32)
    return c


def _host_vecs(inp):
    v = np.zeros((128, NV), np.float32)
    v[:, V_ANW:V_ANW + 16] = inp["attn_norm_w"][0].reshape(16, 128).T
    v[:, V_FNW:V_FNW + 16] = inp["ffn_norm_w"][0].reshape(16, 128).T
    v[:, V_QNW] = np.tile(inp["q_norm_w"][0], 2)
    v[:, V_KNW] = np.tile(inp["k_norm_w"][0], 2)
    cw = inp["conv_dw_w"][0]
    v[:, V_CW:V_CW + 248] = cw.reshape(CK, 8, 128).transpose(2, 1, 0).reshape(128, 248)
    v[:, V_CB:V_CB + 8] = inp["conv_dw_b"][0].reshape(8, 128).T
    v[:, V_LNW:V_LNW + 8] = inp["conv_ln_w"][0].reshape(8, 128).T
    v[:, V_LNB:V_LNB + 8] = inp["conv_ln_b"][0].reshape(8, 128).T
    fw = inp["ffn_dw_w"][0]
    v[:, V_FW:V_FW + 264] = fw.reshape(3, 88, 128).transpose(2, 0, 1).reshape(128, 264)
    v[:, V_FB:V_FB + 88] = inp["ffn_dw_b"][0].reshape(88, 128).T
    sk = inp["sinks"][0]
    for pp in range(2):
        for j in range(4):
            v[0:64, V_SINK + pp * 4 + j] = sk[8 * pp + j]
            v[64:128, V_SINK + pp * 4 + j] = sk[8 * pp + 4 + j]
    return v


_CACHE = {}


def kernel(**inputs):
    inp = {k: np.asarray(v) for k, v in inputs.items()}
    nunits = int(os.environ.get("MK_NUNITS", "8"))
    x = inp["x"].astype(np.float32, copy=False)
    if nunits not in _CACHE:
        _CACHE[nunits] = build_program(nunits)[0]
    nc = _CACHE[nunits]
    cst = _host_consts()
    vecs = _host_vecs(inp)
    w_in = np.ascontiguousarray(inp["w_in"][0])
    w_out = np.ascontiguousarray(inp["w_out"][0])
    w_up = np.ascontiguousarray(inp["w_up"][0])
    w_down = np.ascontiguousarray(inp["w_down"][0])
    in_maps = []
    for c in range(NCORES):
        b, q = divmod(c, 4)
        s0 = q * TOK
        xe = np.zeros((HALO + TOK, D), np.float32)
        if q > 0:
            xe[0:HALO] = x[b, s0 - HALO:s0]
        xe[HALO:] = x[b, s0:s0 + TOK]
        cc = np.zeros((128, 129), np.float32)
        cc[:, 0:128] = -30000.0 if MASKMODE == "pe" else 0.0
        if q > 0:
            cc[:, 0:128] = cst[:, 512:640]
            cc[:, 128] = 1.0
        in_maps.append({"x_ext": xe, "w_in": w_in, "w_out": w_out, "w_up": w_up, "w_down": w_down,
                        "vecs": vecs, "cst": cst, "ccore": cc})
    res = run_bass_kernel_spmd(nc, in_maps, core_ids=list(range(NCORES)))
    out = np.zeros((2, SEQ, D), np.float32)
    for c in range(NCORES):
        b, q = divmod(c, 4)
        out[b, q * TOK:(q + 1) * TOK] = res.results[c]["y_out"]
    return out
```
